# Optimizing a Trainium2 kernel written in Bass

```python
import math
import jax, jax.numpy as jnp
from jax import lax
import numpy as np

D_MODEL = 2048
BATCH = 1
SEQ = 8192
DEPTH = 1
DEC_BATCH = 4
DEC_SEQ = 8192
PAST_LEN = 128

N_HEADS = 16
QK_NOPE_DIM = 128
QK_ROPE_DIM = 64
V_HEAD_DIM = 128
Q_LORA_RANK = 512
KV_LORA_RANK = 512
ROPE_BASE = 10000.0
Q_BLOCK = 128
LRU_WIDTH = D_MODEL
LRU_BLOCKS = 16
LRU_BLOCK_W = LRU_WIDTH // LRU_BLOCKS
CONV_WIDTH = 4
CONV_LEFT = 2
LRU_C = 8.0
D_FF = ((8 * D_MODEL // 3 + 255) // 256) * 256
RMS_EPS = 1e-6
N_IN = Q_LORA_RANK + KV_LORA_RANK + QK_ROPE_DIM + 2 * LRU_WIDTH + 2 * D_MODEL
SPLIT_POINTS = tuple(int(v) for v in np.cumsum(
    [Q_LORA_RANK, KV_LORA_RANK, QK_ROPE_DIM, LRU_WIDTH, LRU_WIDTH]))

kernel_name = "hybrid_mla_rglru_macaron_encoder"


def rms_norm(x, g):
    xf = x.astype(jnp.float32)
    y = xf * lax.rsqrt(jnp.mean(xf * xf, axis=-1, keepdims=True) + RMS_EPS)
    return (y * g.astype(jnp.float32)).astype(x.dtype)


def swiglu_ffn(x, g, w_gate, w_up, w_down):
    h = rms_norm(x, g)
    return (jax.nn.silu(h @ w_gate) * (h @ w_up)) @ w_down


def rope_tables(seq_len, dtype):
    pos = jnp.arange(seq_len, dtype=jnp.float32)
    inv_freq = ROPE_BASE ** (-jnp.arange(0, QK_ROPE_DIM, 2, dtype=jnp.float32) / QK_ROPE_DIM)
    ang = pos[:, None] * inv_freq[None, :]
    return jnp.cos(ang).astype(dtype), jnp.sin(ang).astype(dtype)


def apply_rope(x, cos, sin):
    half = QK_ROPE_DIM // 2
    x1, x2 = x[..., :half], x[..., half:]
    return jnp.concatenate([x1 * cos - x2 * sin, x2 * cos + x1 * sin], axis=-1)


def mla_attention(q_nope, q_pe, k_nope, k_pe, v):
    B, S = q_nope.shape[0], q_nope.shape[1]
    nq = S // Q_BLOCK
    scale = (QK_NOPE_DIM + QK_ROPE_DIM) ** -0.5
    qn = q_nope.reshape(B, nq, Q_BLOCK, N_HEADS, QK_NOPE_DIM).transpose(1, 0, 2, 3, 4)
    qp = q_pe.reshape(B, nq, Q_BLOCK, N_HEADS, QK_ROPE_DIM).transpose(1, 0, 2, 3, 4)

    def block(args):
        qn_b, qp_b = args
        s = (jnp.einsum('bqhd,bkhd->bhqk', qn_b, k_nope)
             + jnp.einsum('bqhr,bkr->bhqk', qp_b, k_pe)).astype(jnp.float32) * scale
        p = jax.nn.softmax(s, axis=-1).astype(v.dtype)
        return jnp.einsum('bhqk,bkhd->bqhd', p, v)

    o = lax.map(block, (qn, qp))
    return o.transpose(1, 0, 2, 3, 4).reshape(B, S, N_HEADS * V_HEAD_DIM)


def centred_depthwise_conv(x, w, b):
    S = x.shape[1]
    xp = jnp.pad(x, ((0, 0), (CONV_LEFT, CONV_WIDTH - 1 - CONV_LEFT), (0, 0)))
    y = b
    for k in range(CONV_WIDTH):
        y = y + xp[:, k:k + S] * w[k]
    return y


def _lin_combine(e1, e2):
    a1, b1 = e1
    a2, b2 = e2
    return a1 * a2, a2 * b1 + b2


def rglru_direction(xc, w_a, b_a, w_i, b_i, lam, reverse):
    B, S, W = xc.shape
    xb = xc.reshape(B, S, LRU_BLOCKS, LRU_BLOCK_W)
    r = jax.nn.sigmoid((jnp.einsum('bsnk,nkj->bsnj', xb, w_a).reshape(B, S, W) + b_a).astype(jnp.float32))
    i = jax.nn.sigmoid((jnp.einsum('bsnk,nkj->bsnj', xb, w_i).reshape(B, S, W) + b_i).astype(jnp.float32))
    log_a = -LRU_C * r * jax.nn.softplus(-lam.astype(jnp.float32))
    a = jnp.exp(log_a)
    u = jnp.sqrt(-jnp.expm1(2.0 * log_a)) * (i * xc.astype(jnp.float32))
    _, h = lax.associative_scan(_lin_combine, (a, u), axis=1, reverse=reverse)
    return h


def gated_parallel_mixer(x, mix_norm, w_in, q_norm, w_uq, kv_norm, w_ukv, w_o_attn,
                         conv_w, conv_b, rg_w_a, rg_b_a, rg_w_i, rg_b_i, rg_lambda,
                         w_o_rec, w_out):
    B, S, _ = x.shape
    h = rms_norm(x, mix_norm)
    proj = h @ w_in
    c_q, c_kv, k_pe, x_rec, g_rec, gates = jnp.split(proj, SPLIT_POINTS, axis=-1)

    q = (rms_norm(c_q, q_norm) @ w_uq).reshape(B, S, N_HEADS, QK_NOPE_DIM + QK_ROPE_DIM)
    q_nope, q_pe = q[..., :QK_NOPE_DIM], q[..., QK_NOPE_DIM:]
    kv = (rms_norm(c_kv, kv_norm) @ w_ukv).reshape(B, S, N_HEADS, QK_NOPE_DIM + V_HEAD_DIM)
    k_nope, v = kv[..., :QK_NOPE_DIM], kv[..., QK_NOPE_DIM:]
    cos, sin = rope_tables(S, x.dtype)
    q_pe = apply_rope(q_pe, cos[:, None, :], sin[:, None, :])
    k_pe = apply_rope(k_pe, cos, sin)
    y_attn = mla_attention(q_nope, q_pe, k_nope, k_pe, v) @ w_o_attn

    xc = centred_depthwise_conv(x_rec, conv_w, conv_b)
    h_rec = (rglru_direction(xc, rg_w_a[0], rg_b_a[0], rg_w_i[0], rg_b_i[0], rg_lambda[0], False)
             + rglru_direction(xc, rg_w_a[1], rg_b_a[1], rg_w_i[1], rg_b_i[1], rg_lambda[1], True))
    y_rec = (jax.nn.gelu(g_rec) * h_rec.astype(x.dtype)) @ w_o_rec

    g_attn, g_r = jnp.split(jax.nn.sigmoid(gates), 2, axis=-1)
    return (g_attn * y_attn + g_r * y_rec) @ w_out


def encoder(x, ffn1_norm, ffn1_w_gate, ffn1_w_up, ffn1_w_down,
            mix_norm, w_in, q_norm, w_uq, kv_norm, w_ukv, w_o_attn,
            conv_w, conv_b, rg_w_a, rg_b_a, rg_w_i, rg_b_i, rg_lambda, w_o_rec, w_out,
            ffn2_norm, ffn2_w_gate, ffn2_w_up, ffn2_w_down, final_norm):
    for l in range(DEPTH):
        x = x + 0.5 * swiglu_ffn(x, ffn1_norm[l], ffn1_w_gate[l], ffn1_w_up[l], ffn1_w_down[l])
        x = x + gated_parallel_mixer(
            x, mix_norm[l], w_in[l], q_norm[l], w_uq[l], kv_norm[l], w_ukv[l], w_o_attn[l],
            conv_w[l], conv_b[l], rg_w_a[l], rg_b_a[l], rg_w_i[l], rg_b_i[l], rg_lambda[l],
            w_o_rec[l], w_out[l])
        x = x + 0.5 * swiglu_ffn(x, ffn2_norm[l], ffn2_w_gate[l], ffn2_w_up[l], ffn2_w_down[l])
    return rms_norm(x, final_norm)


def setup_inputs(seed: int = 0) -> dict:
    key = jax.random.key(seed)
    ks = jax.random.split(key, 32)
    f32 = jnp.float32

    def nrm(k, shape, fan_in):
        return jax.random.normal(k, shape, f32) * (fan_in ** -0.5)

    def gain(k, shape):
        return 1.0 + 0.02 * jax.random.normal(k, shape, f32)

    def bias(k, shape):
        return 0.02 * jax.random.normal(k, shape, f32)

    L = DEPTH
    u = jax.random.uniform(ks[20], (L, 2, LRU_WIDTH), f32, minval=0.9, maxval=0.999)
    s = u ** (1.0 / LRU_C)
    rg_lambda = jnp.log(s) - jnp.log1p(-s)
    return {
        "x_prompt": jax.random.normal(ks[0], (BATCH, SEQ, D_MODEL), f32),
        "x_sample": jax.random.normal(ks[1], (DEC_BATCH, DEC_SEQ, D_MODEL), f32),
        "ffn1_norm": gain(ks[2], (L, D_MODEL)),
        "ffn1_w_gate": nrm(ks[3], (L, D_MODEL, D_FF), D_MODEL),
        "ffn1_w_up": nrm(ks[4], (L, D_MODEL, D_FF), D_MODEL),
        "ffn1_w_down": nrm(ks[5], (L, D_FF, D_MODEL), D_FF),
        "mix_norm": gain(ks[6], (L, D_MODEL)),
        "w_in": nrm(ks[7], (L, D_MODEL, N_IN), D_MODEL),
        "q_norm": gain(ks[8], (L, Q_LORA_RANK)),
        "w_uq": nrm(ks[9], (L, Q_LORA_RANK, N_HEADS * (QK_NOPE_DIM + QK_ROPE_DIM)), Q_LORA_RANK),
        "kv_norm": gain(ks[10], (L, KV_LORA_RANK)),
        "w_ukv": nrm(ks[11], (L, KV_LORA_RANK, N_HEADS * (QK_NOPE_DIM + V_HEAD_DIM)), KV_LORA_RANK),
        "w_o_attn": nrm(ks[12], (L, N_HEADS * V_HEAD_DIM, D_MODEL), N_HEADS * V_HEAD_DIM),
        "conv_w": nrm(ks[13], (L, CONV_WIDTH, LRU_WIDTH), CONV_WIDTH),
        "conv_b": bias(ks[14], (L, LRU_WIDTH)),
        "rg_w_a": nrm(ks[15], (L, 2, LRU_BLOCKS, LRU_BLOCK_W, LRU_BLOCK_W), LRU_BLOCK_W),
        "rg_b_a": bias(ks[16], (L, 2, LRU_WIDTH)),
        "rg_w_i": nrm(ks[17], (L, 2, LRU_BLOCKS, LRU_BLOCK_W, LRU_BLOCK_W), LRU_BLOCK_W),
        "rg_b_i": bias(ks[18], (L, 2, LRU_WIDTH)),
        "rg_lambda": rg_lambda,
        "w_o_rec": nrm(ks[19], (L, LRU_WIDTH, D_MODEL), LRU_WIDTH),
        "w_out": nrm(ks[21], (L, D_MODEL, D_MODEL), D_MODEL),
        "ffn2_norm": gain(ks[22], (L, D_MODEL)),
        "ffn2_w_gate": nrm(ks[23], (L, D_MODEL, D_FF), D_MODEL),
        "ffn2_w_up": nrm(ks[24], (L, D_MODEL, D_FF), D_MODEL),
        "ffn2_w_down": nrm(ks[25], (L, D_FF, D_MODEL), D_FF),
        "final_norm": gain(ks[26], (D_MODEL,)),
    }


def reference(x_prompt, x_sample, ffn1_norm, ffn1_w_gate, ffn1_w_up, ffn1_w_down,
              mix_norm, w_in, q_norm, w_uq, kv_norm, w_ukv, w_o_attn,
              conv_w, conv_b, rg_w_a, rg_b_a, rg_w_i, rg_b_i, rg_lambda, w_o_rec, w_out,
              ffn2_norm, ffn2_w_gate, ffn2_w_up, ffn2_w_down, final_norm):
    y_prompt = encoder(x_prompt, ffn1_norm, ffn1_w_gate, ffn1_w_up, ffn1_w_down,
                       mix_norm, w_in, q_norm, w_uq, kv_norm, w_ukv, w_o_attn,
                       conv_w, conv_b, rg_w_a, rg_b_a, rg_w_i, rg_b_i, rg_lambda, w_o_rec, w_out,
                       ffn2_norm, ffn2_w_gate, ffn2_w_up, ffn2_w_down, final_norm)
    y_sample = encoder(x_sample, ffn1_norm, ffn1_w_gate, ffn1_w_up, ffn1_w_down,
                       mix_norm, w_in, q_norm, w_uq, kv_norm, w_ukv, w_o_attn,
                       conv_w, conv_b, rg_w_a, rg_b_a, rg_w_i, rg_b_i, rg_lambda, w_o_rec, w_out,
                       ffn2_norm, ffn2_w_gate, ffn2_w_up, ffn2_w_down, final_norm)
    return (y_prompt, y_sample)
```

```python
import contextlib
import numpy as np
import ml_dtypes
import concourse.bass as bass
import concourse.mybir as mybir
from concourse.bass_utils import run_bass_kernel_spmd

F32 = mybir.dt.float32
BF16 = mybir.dt.bfloat16
AF = mybir.ActivationFunctionType
ALU = mybir.AluOpType

D = 2048
DC = 16
FF = 5632
FC = 44
T = 512
NH = 16
EPS = 1e-6
N_IN = 9280
SCALE = 192.0 ** -0.5
NSLOT = 3
SLOT = 8192

V_G1, V_GM, V_GQ, V_GKV, V_G2, V_GF = 0, 16, 32, 36, 40, 56
V_CW, V_CB, V_BA, V_BI, V_LAM = 72, 136, 152, 184, 216
NV = 248


class DSem:
    def __init__(self, sem, key):
        self.sem, self.key, self.total = sem, key, 0


class Buf:
    __slots__ = ("lw", "rd", "dsem")

    def __init__(self):
        self.lw = None
        self.rd = {}
        self.dsem = None


class Eng:
    def __init__(self, name, eng, sem):
        self.name, self.eng, self.sem = name, eng, sem
        self.count = 0
        self.seen = {}

    def need(self, ev):
        if ev is None:
            return
        key, obj, v = ev
        if isinstance(obj, DSem):
            v = obj.total
            sem = obj.sem
        else:
            if key == "pe" and self.name == "pe":
                return
            sem = obj
        if self.seen.get(key, 0) >= v:
            return
        self.eng.wait_ge(sem, v)
        self.seen[key] = v


class K:
    def __init__(self, nc, es):
        self.nc = nc
        self.es = es
        self.pe = Eng("pe", nc.tensor, es.enter_context(nc.semaphore("s_pe")))
        self.act = Eng("act", nc.scalar, es.enter_context(nc.semaphore("s_act")))
        self.dve = Eng("dve", nc.vector, es.enter_context(nc.semaphore("s_dve")))
        self.pool = Eng("pool", nc.gpsimd, es.enter_context(nc.semaphore("s_pool")))
        self.sp = Eng("sp", nc.sync, None)
        self.engs = [self.pe, self.act, self.dve, self.pool, self.sp]
        self.dsems = []
        self.bufs = []

    def buf(self):
        b = Buf()
        self.bufs.append(b)
        return b

    def bufs_n(self, n):
        return [self.buf() for _ in range(n)]

    def dsem(self, name):
        d = DSem(self.es.enter_context(self.nc.semaphore("d_" + name)), "d_" + name + str(len(self.dsems)))
        self.dsems.append(d)
        return d

    def op(self, E, fn, reads=(), writes=(), inc=True):
        for b in reads:
            E.need(b.lw)
        for b in writes:
            E.need(b.lw)
            for ev in b.rd.values():
                E.need(ev)
        ins = fn()
        if inc:
            E.count += 1
            ins.then_inc(E.sem, 1)
            v = E.count
        else:
            v = E.count + 1
        ev = (E.name, E.sem, v)
        for b in reads:
            b.rd[E.name] = ev
        for b in writes:
            b.lw = ev
            b.rd = {}
        return ins

    def dma(self, Q, ds, out, in_, reads=(), writes=()):
        for b in reads:
            Q.need(b.lw)
        for b in writes:
            Q.need(b.lw)
            for ev in b.rd.values():
                Q.need(ev)
        ds.total += 16
        Q.eng.dma_start(out=out, in_=in_).then_inc(ds.sem, 16)
        ev = (ds.key, ds, ds.total)
        for b in reads:
            b.rd[ds.key] = ev
        for b in writes:
            b.lw = ev
            b.rd = {}

    def barrier(self):
        for F in self.engs:
            for E in self.engs:
                if E is F or E.sem is None:
                    continue
                if E.count > 0:
                    F.need((E.name, E.sem, E.count))
            for d in self.dsems:
                if d.total > 0:
                    F.need((d.key, d, d.total))
        for b in self.bufs:
            b.lw = None
            b.rd = {}


class WStream:
    def __init__(self, k, ring, ring_b, ring_d):
        self.k, self.ring, self.ring_b, self.ring_d = k, ring, ring_b, ring_d
        self.blocks = []
        self.issued = 0
        self.base = 0

    def add(self, ap2d, n):
        self.blocks.append((ap2d, n))
        return len(self.blocks) - 1

    def fetch(self, i, n=1):
        k = self.k
        assert i == self.base, (i, self.base)
        self.base = i + n
        hi = min(i + NSLOT - 1, len(self.blocks) - 1)
        while self.issued <= hi:
            j = self.issued
            ap2d, ne = self.blocks[j]
            s = j % NSLOT
            k.dma(k.sp, self.ring_d[s], self.ring[s][:, 0:ne], ap2d, writes=[self.ring_b[s]])
            self.issued += 1
        out = []
        for j in range(i, i + n):
            s = j % NSLOT
            out += [self.ring[s], self.ring_b[s]]
        return out


def build(S, debug=False, stop=None):
    NT = S // T
    NKC = S // 128
    PT = min(1024, S)
    NSUB = PT // 512
    NP = S // PT
    nc = bass.Bass("TRN2", target_bir_lowering=False)

    def din(name, shape, dt=F32):
        return nc.dram_tensor(name, list(shape), dt, kind="ExternalInput").ap()

    def dscr(name, shape, dt):
        kind = "ExternalOutput" if debug else "Internal"
        return nc.dram_tensor(name, list(shape), dt, kind=kind).ap()

    xT_in = din("xT", [D, S])
    w_g = [din("ffn1_w_gate", [D, FF]), din("ffn2_w_gate", [D, FF])]
    w_u = [din("ffn1_w_up", [D, FF]), din("ffn2_w_up", [D, FF])]
    w_d = [din("ffn1_w_down", [FF, D]), din("ffn2_w_down", [FF, D])]
    w_in = din("w_in", [D, N_IN])
    w_kpesw = din("w_kpesw", [D, 64])
    w_uq = din("w_uq", [512, 3072])
    w_uqsw = din("w_uqsw", [512, 1024])
    w_ukv = din("w_ukv", [512, 4096])
    w_oa = din("w_o_attn", [D, D])
    w_or = din("w_o_rec", [D, D])
    w_out = din("w_out", [D, D])
    rg_wa = din("rg_w_a", [2, 16, 128, 128])
    rg_wi = din("rg_w_i", [2, 16, 128, 128])
    vecs_in = din("vecs", [128, NV])
    ropeC_in = din("ropeC", [64, S])
    ropeS_in = din("ropeS", [64, S])
    yT_out = nc.dram_tensor("yT", [D, S], F32, kind="ExternalOutput").ap()

    s_wg = [dscr("s_wg%d" % i, [11, 128, SLOT], BF16) for i in range(2)]
    s_wu = [dscr("s_wu%d" % i, [11, 128, SLOT], BF16) for i in range(2)]
    s_wd = [dscr("s_wd%d" % i, [16, 128, FC * 128], BF16) for i in range(2)]
    s_win = dscr("s_win", [18, 128, SLOT], BF16)
    s_wkpe = dscr("s_wkpe", [128, 16 * 128], BF16)
    s_wuq = dscr("s_wuq", [2, 128, SLOT], BF16)
    s_wukv = dscr("s_wukv", [2, 128, SLOT], BF16)
    s_woa = dscr("s_woa", [4, 128, SLOT], BF16)
    s_wor = dscr("s_wor", [4, 128, SLOT], BF16)
    s_wout = dscr("s_wout", [4, 128, SLOT], BF16)
    X1T = dscr("X1T", [NT, 128, DC * T], F32)
    QN = dscr("QN", [NT, 128, NH * T], BF16)
    QPE = dscr("QPE", [NT, 64, NH * T], BF16)
    KT = dscr("KT", [NH, 128, S], BF16)
    VS = dscr("VS", [NH, 128, NKC * 128], BF16)
    KPE = dscr("KPE", [64, S], BF16)
    XREC = dscr("XREC", [NT, 128, DC, T], F32)
    GREC = dscr("GREC", [NT, 128, DC, T], F32)
    GATES = dscr("GATES", [NT, 128, 32, T], F32)
    HG = dscr("HG", [NT, 128, DC * T], BF16)
    ATT = dscr("ATT", [NT, 128, NH * T], BF16)

    es = contextlib.ExitStack()
    with es:
        k = K(nc, es)
        pe, act, dve, pool, sp = k.pe, k.act, k.dve, k.pool, k.sp

        def sb(name, shape, dt):
            return es.enter_context(nc.sbuf_tensor(name, list(shape), dt))

        PSt = [es.enter_context(nc.psum_tensor("ps%d" % i, [128, 512], F32)) for i in range(8)]
        PSb = k.bufs_n(8)

        vecs = sb("vecs_sb", [128, NV], F32)
        vecs_b = k.buf()
        ones = sb("ones", [128, 128], BF16)
        ones_b = k.buf()
        epsc = sb("epsc", [128, 1], F32)
        epsc_b = k.buf()
        d_misc = k.dsem("misc")
        k.dma(sp, d_misc, vecs[:], vecs_in[:, :], writes=[vecs_b])
        k.op(dve, lambda: nc.vector.memset(ones[:], 1.0), writes=[ones_b])
        k.op(dve, lambda: nc.vector.memset(epsc[:], EPS), writes=[epsc_b])
        ones_f = sb("ones_f", [128, 128], F32)
        onesf_b = k.buf()
        k.op(dve, lambda: nc.vector.memset(ones_f[:], 1.0), writes=[onesf_b])

        d_prep = k.dsem("prep")

        def prep(dst3, src3):
            k.dma(pool, d_prep, dst3, src3)

        def v3(ap2d, c, f):
            return ap2d.rearrange("p (c f) -> p c f", c=c, f=f)

        def prep_ffn(i):
            gv = w_g[i].rearrange("(c p) (b f) -> b p c f", p=128, f=512)
            uv = w_u[i].rearrange("(c p) (b f) -> b p c f", p=128, f=512)
            for b in range(11):
                prep(v3(s_wg[i][b], 16, 512), gv[b])
                prep(v3(s_wu[i][b], 16, 512), uv[b])
            dv = w_d[i].rearrange("(c p) (b f) -> b p c f", p=128, f=128)
            for b in range(16):
                prep(v3(s_wd[i][b], FC, 128), dv[b])

        prep_ffn(0)
        win_v = w_in.rearrange("(c p) n -> p c n", p=128)
        in_cols = [1088 + 512 * b for b in range(16)] + [0, 512]
        for b, c0 in enumerate(in_cols):
            prep(v3(s_win[b], 16, 512), win_v[:, :, c0:c0 + 512])
        kp3 = v3(s_wkpe, 16, 128)
        prep(kp3[:, :, 0:64], win_v[:, :, 1024:1088])
        prep(kp3[:, :, 64:128], w_kpesw.rearrange("(c p) n -> p c n", p=128))
        uq4 = w_uq.rearrange("(c p) (h e) -> p c h e", p=128, e=192)
        b0 = s_wuq[0].rearrange("p (c h e) -> p c h e", c=4, h=16, e=128)
        b1 = s_wuq[1].rearrange("p (c x) -> p c x", c=4, x=2048)
        for c4 in range(4):
            prep(b0[:, c4], uq4[:, c4, :, 0:128])
            prep(b1[:, c4, 0:1024].rearrange("p (h e) -> p h e", h=16, e=64), uq4[:, c4, :, 128:192])
        prep(b1[:, :, 1024:2048], w_uqsw.rearrange("(c p) n -> p c n", p=128))
        ukv4 = w_ukv.rearrange("(c p) (h e) -> p c h e", p=128, e=256)
        for c4 in range(4):
            prep(s_wukv[0].rearrange("p (c h e) -> p c h e", c=4, h=16, e=128)[:, c4], ukv4[:, c4, :, 0:128])
            prep(s_wukv[1].rearrange("p (c h e) -> p c h e", c=4, h=16, e=128)[:, c4], ukv4[:, c4, :, 128:256])
        k.barrier()

        def prep_group_b():
            for sw, w in ((s_woa, w_oa), (s_wor, w_or), (s_wout, w_out)):
                wv = w.rearrange("(c p) (b f) -> b p c f", p=128, f=512)
                for b in range(4):
                    prep(v3(sw[b], 16, 512), wv[b])
            prep_ffn(1)
        if stop == "p0":
            return nc

        es1 = contextlib.ExitStack()

        def sb1(name, shape, dt):
            return es1.enter_context(nc.sbuf_tensor(name, list(shape), dt))

        xT = sb1("xTt", [128, DC, T], F32)
        xT_b = k.bufs_n(DC)
        hT = sb1("hTt", [128, DC, T], BF16)
        hT_b = k.bufs_n(DC)
        mid = sb1("midt", [128, FC, T], BF16)
        mid_b = k.bufs_n(FC)
        ring = [sb1("ring%d" % i, [128, SLOT], BF16) for i in range(NSLOT)]
        ring_b = k.bufs_n(NSLOT)
        ring_d = [k.dsem("ring%d" % i) for i in range(NSLOT)]
        sq = [sb1("sq%d" % i, [128, T], BF16) for i in range(2)]
        sq_b = k.bufs_n(2)
        sd = sb1("sd", [128, T], F32)
        sd_b = k.buf()
        rstd = sb1("rstd", [128, T], F32)
        rstd_b = k.buf()
        stg = [sb1("stg%d" % i, [128, T], F32) for i in range(4)]
        stg_b = k.bufs_n(4)
        stg_d = [k.dsem("stg%d" % i) for i in range(4)]
        tmp = [sb1("tmp%d" % i, [128, T], F32) for i in range(4)]
        tmp_b = k.bufs_n(4)
        sgb = [sb1("sgb%d" % i, [128, T], F32) for i in range(4)]
        sgb_b = k.bufs_n(4)
        ropeC = sb1("ropeC_sb", [64, T], F32)
        ropeS = sb1("ropeS_sb", [64, T], F32)
        rope_b = k.buf()
        d_rope = k.dsem("rope")
        d_x = k.dsem("x")
        d_st = k.dsem("st")
        d_ld = k.dsem("ld")

        ctr = {"ps": 0, "stg": 0, "tmp": 0, "sq": 0, "pp": 0}

        def rmsnorm(srcs, src_bufs, gcol, dn, outs, out_bufs, ps_sum):
            C = len(srcs)
            for c in range(C):
                i = ctr["sq"] % 2
                ctr["sq"] += 1
                k.op(act, lambda c=c, i=i: nc.scalar.activation(out=sq[i][:], in_=srcs[c], func=AF.Square),
                     reads=[src_bufs[c]], writes=[sq_b[i]])
                k.op(pe, lambda c=c, i=i: nc.tensor.matmul(PSt[ps_sum][:], ones[:], sq[i][:], start=(c == 0),
                                                           stop=(c == C - 1)),
                     reads=[sq_b[i], ones_b], writes=([PSb[ps_sum]] if (c == 0 or c == C - 1) else ()))
            k.op(act, lambda: nc.scalar.activation(out=sd[:], in_=PSt[ps_sum][:], func=AF.Sqrt,
                                                   bias=epsc[:, 0:1], scale=1.0 / dn),
                 reads=[PSb[ps_sum], epsc_b], writes=[sd_b])
            k.op(dve, lambda: nc.vector.reciprocal(out=rstd[:], in_=sd[:]), reads=[sd_b], writes=[rstd_b])
            for c in range(C):
                k.op(dve, lambda c=c: nc.vector.scalar_tensor_tensor(
                    out=outs[c], in0=srcs[c], scalar=vecs[:, gcol + c:gcol + c + 1], in1=rstd[:],
                    op0=ALU.mult, op1=ALU.mult),
                    reads=[src_bufs[c], rstd_b, vecs_b], writes=[out_bufs[c]])

        def mmgroup(ps, items, m=128):
            n = len(items)
            for i, (l, r, rb) in enumerate(items):
                first, last = (i == 0), (i == n - 1)
                k.op(pe, lambda l=l, r=r, first=first, last=last: nc.tensor.matmul(
                    PSt[ps][0:m, :], l, r, start=first, stop=last),
                    reads=rb, writes=([PSb[ps]] if (first or last) else ()), inc=last)

        def ffn(ws, i_g, i_u, i_d, gcol):
            rmsnorm([xT[:, c, :] for c in range(DC)], xT_b, gcol, float(D),
                    [hT[:, c, :] for c in range(DC)], hT_b, 6)
            for fb in range(11):
                wg, wgb = ws.fetch(i_g[fb])
                wg3 = wg[:].rearrange("p (c f) -> p c f", c=16, f=512)
                for j in range(4):
                    fc = fb * 4 + j
                    pg = fc % 2
                    mmgroup(pg, [(wg3[:, c, j * 128:(j + 1) * 128], hT[:, c, :], [wgb, hT_b[c]]) for c in range(DC)])
                    k.op(act, lambda pg=pg, j=j: nc.scalar.activation(out=sgb[j][:], in_=PSt[pg][:], func=AF.Silu),
                         reads=[PSb[pg]], writes=[sgb_b[j]])
                wu, wub = ws.fetch(i_u[fb])
                wu3 = wu[:].rearrange("p (c f) -> p c f", c=16, f=512)
                for j in range(4):
                    fc = fb * 4 + j
                    pu = 2 + fc % 2
                    mmgroup(pu, [(wu3[:, c, j * 128:(j + 1) * 128], hT[:, c, :], [wub, hT_b[c]]) for c in range(DC)])
                    k.op(dve, lambda pu=pu, j=j, fc=fc: nc.vector.tensor_tensor(
                        out=mid[:, fc, :], in0=PSt[pu][:], in1=sgb[j][:], op=ALU.mult),
                        reads=[PSb[pu], sgb_b[j]], writes=[mid_b[fc]])
            for oc in range(DC):
                wd, wdb = ws.fetch(i_d[oc])
                wd3 = wd[:, 0:FC * 128].rearrange("p (c f) -> p c f", c=FC, f=128)
                pd = 4 + oc % 2
                mmgroup(pd, [(wd3[:, fc, :], mid[:, fc, :], [wdb, mid_b[fc]]) for fc in range(FC)])
                k.op(dve, lambda pd=pd, oc=oc: nc.vector.scalar_tensor_tensor(
                    out=xT[:, oc, :], in0=PSt[pd][:], scalar=0.5, in1=xT[:, oc, :], op0=ALU.mult, op1=ALU.add),
                    reads=[PSb[pd], xT_b[oc]], writes=[xT_b[oc]])

        def stage_store(dst_ap, fn_make, reads):
            i = ctr["stg"] % 4
            ctr["stg"] += 1
            fn_make(stg[i], stg_b[i], reads)
            k.dma(pool, stg_d[i], dst_ap, stg[i][:], reads=[stg_b[i]])

        ws1 = WStream(k, ring, ring_b, ring_d)
        p1_idx = []
        for t in range(NT):
            ig = [None] * 11
            iu = [None] * 11
            for b in range(11):
                ig[b] = ws1.add(s_wg[0][b], SLOT)
                iu[b] = ws1.add(s_wu[0][b], SLOT)
            idn = [ws1.add(s_wd[0][b], FC * 128) for b in range(16)]
            iin = [ws1.add(s_win[b], SLOT) for b in range(17)]
            iuq = [ws1.add(s_wuq[b], SLOT) for b in range(2)]
            iin.append(ws1.add(s_win[17], SLOT))
            ikpe = ws1.add(s_wkpe, 16 * 128)
            iukv = [ws1.add(s_wukv[b], SLOT) for b in range(2)]
            p1_idx.append((ig, iu, idn, iin, ikpe, iuq, iukv))

        for t in range(NT):
            ig, iu, idn, iin, ikpe, iuq, iukv = p1_idx[t]
            tok = slice(t * T, (t + 1) * T)
            k.dma(pool, d_x, xT[:], xT_in.rearrange("(c p) s -> p c s", p=128)[:, :, tok], writes=xT_b)
            k.dma(pool, d_rope, ropeC[:], ropeC_in[:, tok], writes=[rope_b])
            k.dma(pool, d_rope, ropeS[:], ropeS_in[:, tok], writes=[rope_b])
            if t == 0:
                prep_group_b()
            ffn(ws1, ig, iu, idn, V_G1)
            k.dma(pool, d_x, X1T[t].rearrange("p (c s) -> p c s", c=DC), xT[:], reads=xT_b)
            rmsnorm([xT[:, c, :] for c in range(DC)], xT_b, V_GM, float(D),
                    [hT[:, c, :] for c in range(DC)], hT_b, 6)

            def inproj_chunk(w3, wb, j, ps, m=128):
                mmgroup(ps, [(w3[:, c, j * m:(j + 1) * m], hT[:, c, :], [wb, hT_b[c]]) for c in range(DC)], m=m)

            for blk in range(16):
                w, wb = ws1.fetch(iin[blk])
                w3 = w[:].rearrange("p (c f) -> p c f", c=16, f=512)
                for j in range(4):
                    ps = ctr["ps"] % 4
                    ctr["ps"] += 1
                    inproj_chunk(w3, wb, j, ps)
                    ch = (blk % 4) * 4 + j if blk < 8 else (blk - 8) * 4 + j
                    if blk < 4:
                        def mk(st, stb, rd, ps=ps):
                            k.op(act, lambda: nc.scalar.activation(out=st[:], in_=PSt[ps][:], func=AF.Copy),
                                 reads=[PSb[ps]], writes=[stb])
                        stage_store(XREC[t][:, ch, :], mk, None)
                    elif blk < 8:
                        def mk(st, stb, rd, ps=ps):
                            ti = ctr["tmp"] % 4
                            ctr["tmp"] += 1
                            k.op(act, lambda: nc.scalar.activation(out=tmp[ti][:], in_=PSt[ps][:], func=AF.Square),
                                 reads=[PSb[ps]], writes=[tmp_b[ti]])
                            k.op(dve, lambda: nc.vector.tensor_scalar(
                                out=tmp[ti][:], in0=tmp[ti][:], scalar1=0.044715, scalar2=1.0,
                                op0=ALU.mult, op1=ALU.add), reads=[tmp_b[ti]], writes=[tmp_b[ti]])
                            k.op(dve, lambda: nc.vector.tensor_tensor(
                                out=tmp[ti][:], in0=PSt[ps][:], in1=tmp[ti][:], op=ALU.mult),
                                reads=[PSb[ps], tmp_b[ti]], writes=[tmp_b[ti]])
                            k.op(act, lambda: nc.scalar.activation(out=tmp[ti][:], in_=tmp[ti][:], func=AF.Sigmoid,
                                                                   scale=1.5957691216057308),
                                 reads=[tmp_b[ti]], writes=[tmp_b[ti]])
                            k.op(dve, lambda: nc.vector.tensor_tensor(
                                out=st[:], in0=PSt[ps][:], in1=tmp[ti][:], op=ALU.mult),
                                reads=[PSb[ps], tmp_b[ti]], writes=[stb])
                        stage_store(GREC[t][:, ch, :], mk, None)
                    else:
                        def mk(st, stb, rd, ps=ps):
                            k.op(act, lambda: nc.scalar.activation(out=st[:], in_=PSt[ps][:], func=AF.Sigmoid),
                                 reads=[PSb[ps]], writes=[stb])
                        stage_store(GATES[t][:, ch, :], mk, None)

            w, wb = ws1.fetch(iin[16])
            w3 = w[:].rearrange("p (c f) -> p c f", c=16, f=512)
            for j in range(4):
                inproj_chunk(w3, wb, j, j)
            cqn = [mid[:, 36 + c, :] for c in range(4)]
            cqn_b = [mid_b[36 + c] for c in range(4)]
            rmsnorm([PSt[j][:] for j in range(4)], [PSb[j] for j in range(4)], V_GQ, 512.0, cqn, cqn_b, 6)
            wq0, wq0b, wq1, wq1b = ws1.fetch(iuq[0], 2)
            wq0_3 = wq0[:].rearrange("p (c x) -> p c x", c=4, x=2048)
            wq1_3 = wq1[:].rearrange("p (c x) -> p c x", c=4, x=2048)
            for h in range(NH):
                ps = 4 + h % 2
                mmgroup(ps, [(wq0_3[:, c, h * 128:(h + 1) * 128], cqn[c], [wq0b, cqn_b[c]]) for c in range(4)])
                k.op(act, lambda ps=ps, h=h: nc.scalar.activation(out=mid[:, h, :], in_=PSt[ps][:], func=AF.Copy),
                     reads=[PSb[ps]], writes=[mid_b[h]])
                pa, pb = 0 + 2 * (h % 2), 1 + 2 * (h % 2)
                mmgroup(pa, [(wq1_3[:, c, h * 64:(h + 1) * 64], cqn[c], [wq1b, cqn_b[c]]) for c in range(4)], m=64)
                mmgroup(pb, [(wq1_3[:, c, 1024 + h * 64:1024 + (h + 1) * 64], cqn[c], [wq1b, cqn_b[c]])
                             for c in range(4)], m=64)
                t0, t1 = (ctr["tmp"]) % 4, (ctr["tmp"] + 1) % 4
                ctr["tmp"] += 2
                k.op(dve, lambda pa=pa, t0=t0: nc.vector.tensor_tensor(
                    out=tmp[t0][0:64, :], in0=PSt[pa][0:64, :], in1=ropeC[:], op=ALU.mult),
                    reads=[PSb[pa], rope_b], writes=[tmp_b[t0]])
                k.op(dve, lambda pb=pb, t1=t1: nc.vector.tensor_tensor(
                    out=tmp[t1][0:64, :], in0=PSt[pb][0:64, :], in1=ropeS[:], op=ALU.mult),
                    reads=[PSb[pb], rope_b], writes=[tmp_b[t1]])
                k.op(dve, lambda t0=t0, t1=t1, h=h: nc.vector.tensor_tensor(
                    out=mid[0:64, 16 + h, :], in0=tmp[t0][0:64, :], in1=tmp[t1][0:64, :], op=ALU.add),
                    reads=[tmp_b[t0], tmp_b[t1]], writes=[mid_b[16 + h]])
            k.dma(pool, d_st, QN[t].rearrange("p (h s) -> p h s", h=NH), mid[:, 0:16, :], reads=mid_b[0:16])
            k.dma(pool, d_st, QPE[t].rearrange("p (h s) -> p h s", h=NH), mid[0:64, 16:32, :], reads=mid_b[16:32])

            w, wb = ws1.fetch(iin[17])
            w3 = w[:].rearrange("p (c f) -> p c f", c=16, f=512)
            for j in range(4):
                inproj_chunk(w3, wb, j, j)
            ckn = [mid[:, 40 + c, :] for c in range(4)]
            ckn_b = [mid_b[40 + c] for c in range(4)]
            rmsnorm([PSt[j][:] for j in range(4)], [PSb[j] for j in range(4)], V_GKV, 512.0, ckn, ckn_b, 6)
            wkp, wkpb = ws1.fetch(ikpe)
            wkp3 = wkp[:, 0:2048].rearrange("p (c f) -> p c f", c=16, f=128)
            mmgroup(0, [(wkp3[:, c, 0:64], hT[:, c, :], [wkpb, hT_b[c]]) for c in range(DC)], m=64)
            mmgroup(1, [(wkp3[:, c, 64:128], hT[:, c, :], [wkpb, hT_b[c]]) for c in range(DC)], m=64)
            t0, t1 = (ctr["tmp"]) % 4, (ctr["tmp"] + 1) % 4
            ctr["tmp"] += 2
            k.op(dve, lambda: nc.vector.tensor_tensor(out=tmp[t0][0:64, :], in0=PSt[0][0:64, :], in1=ropeC[:],
                                                      op=ALU.mult), reads=[PSb[0], rope_b], writes=[tmp_b[t0]])
            k.op(dve, lambda: nc.vector.tensor_tensor(out=tmp[t1][0:64, :], in0=PSt[1][0:64, :], in1=ropeS[:],
                                                      op=ALU.mult), reads=[PSb[1], rope_b], writes=[tmp_b[t1]])
            k.op(dve, lambda: nc.vector.tensor_tensor(out=mid[0:64, 35, :], in0=tmp[t0][0:64, :],
                                                      in1=tmp[t1][0:64, :], op=ALU.add),
                 reads=[tmp_b[t0], tmp_b[t1]], writes=[mid_b[35]])
            k.dma(pool, d_st, KPE[:, tok], mid[0:64, 35, :], reads=[mid_b[35]])
            wk, wkb = ws1.fetch(iukv[0])
            wk3 = wk[:].rearrange("p (c x) -> p c x", c=4, x=2048)
            for h in range(NH):
                ps = 4 + h % 2
                mmgroup(ps, [(wk3[:, c, h * 128:(h + 1) * 128], ckn[c], [wkb, ckn_b[c]]) for c in range(4)])
                k.op(act, lambda ps=ps, h=h: nc.scalar.activation(out=mid[:, h, :], in_=PSt[ps][:], func=AF.Copy),
                     reads=[PSb[ps]], writes=[mid_b[h]])
            k.dma(pool, d_st, KT.rearrange("h p s -> p h s")[:, :, tok], mid[:, 0:16, :], reads=mid_b[0:16])
            wv, wvb = ws1.fetch(iukv[1])
            wv3 = wv[:].rearrange("p (c x) -> p c x", c=4, x=2048)
            for tc in range(4):
                for g in range(4):
                    ps = (tc * 4 + g) % 2 + 2
                    mmgroup(ps, [(ckn[c][:, tc * 128:(tc + 1) * 128], wv3[:, c, g * 512:(g + 1) * 512],
                                  [wvb, ckn_b[c]]) for c in range(4)])
                    k.op(dve, lambda ps=ps, tc=tc, g=g: nc.vector.tensor_copy(
                        out=mid[:, 16 + tc * 4 + g, :], in_=PSt[ps][:]),
                        reads=[PSb[ps]], writes=[mid_b[16 + tc * 4 + g]])
                k.dma(pool, d_st,
                      VS.rearrange("h p (kc e) -> p kc h e", e=128)[:, t * 4 + tc, :, :],
                      mid[:, 16 + tc * 4:20 + tc * 4, :].rearrange("p g (hh e) -> p (g hh) e", e=128),
                      reads=mid_b[16 + tc * 4:20 + tc * 4])
        k.barrier()
        es1.close()
        if stop == "p1":
            return nc

        es2 = contextlib.ExitStack()

        def sb2(name, shape, dt):
            return es2.enter_context(nc.sbuf_tensor(name, list(shape), dt))

        xr = sb2("xr", [128, S + 3], F32)
        xr_b = k.buf()
        xc = sb2("xc", [128, S], F32)
        xc_b = k.buf()
        xcb = sb2("xcb", [128, S], BF16)
        xcb_b = k.buf()
        xc_pb = k.bufs_n(NP)
        xcb_pb = k.bufs_n(NP)
        hf = sb2("hf", [128, S], F32)
        hf_b = k.buf()
        rgw = sb2("rgw", [128, 2, 2, 16, 128], BF16)
        rgw_b = k.buf()
        rbL = [sb2("rbuf%d" % i, [128, PT], F32) for i in range(2)]
        rbB = k.bufs_n(2)
        ibL = [sb2("ibuf%d" % i, [128, PT], F32) for i in range(2)]
        ibB = k.bufs_n(2)
        abL = [sb2("abuf%d" % i, [128, PT], F32) for i in range(2)]
        abB = k.bufs_n(2)
        ubL = [sb2("ubuf%d" % i, [128, PT], F32) for i in range(2)]
        ubB = k.bufs_n(2)
        hbL = [sb2("hbuf%d" % i, [128, PT], F32) for i in range(2)]
        hbB = k.bufs_n(2)
        ggL = [sb2("ggbuf%d" % i, [128, PT], F32) for i in range(2)]
        ggB = k.bufs_n(2)
        hgL = [sb2("hgbuf%d" % i, [128, PT], BF16) for i in range(2)]
        hgB = k.bufs_n(2)
        onec = sb2("onec", [128, 1], F32)
        onec_b = k.buf()
        k.op(dve, lambda: nc.vector.memset(onec[:], 1.0), writes=[onec_b])
        d_gg = [k.dsem("gg%d" % i) for i in range(2)]
        d_hg = [k.dsem("hg%d" % i) for i in range(2)]
        carry = sb2("carry", [128, 1], F32)
        carry_b = k.buf()
        cv = sb2("cv", [128, 3, 32], F32)
        cv_b = k.buf()
        sw_ = [sb2("swk%d" % i, [128, 32], F32) for i in range(4)]
        sw_b = k.buf()
        d_p2 = k.dsem("p2")
        d_p2s = k.dsem("p2s")
        d_rgw = k.dsem("rgw")

        for dd in range(2):
            k.dma(pool, d_rgw, rgw[:, 0, dd, :, :], rg_wa[dd].rearrange("n k j -> k n j"), writes=[rgw_b])
            k.dma(pool, d_rgw, rgw[:, 1, dd, :, :], rg_wi[dd].rearrange("n k j -> k n j"), writes=[rgw_b])
        k.op(pool, lambda: nc.gpsimd.memset(xr[:, 0:2], 0.0), writes=[xr_b])
        k.op(pool, lambda: nc.gpsimd.memset(xr[:, S + 2:S + 3], 0.0), writes=[xr_b])
        lam = vecs[:, V_LAM:V_LAM + 32]
        m_, e_, z_, z2_ = sw_[0], sw_[1], sw_[2], sw_[3]
        p_ = cv[:, 2, :]

        def vop(fn):
            k.op(dve, fn, reads=[vecs_b, sw_b, cv_b], writes=[sw_b, cv_b])
        vop(lambda: nc.vector.tensor_scalar(out=m_[:], in0=lam, scalar1=-1.0, scalar2=None, op0=ALU.mult))
        vop(lambda: nc.vector.tensor_tensor(out=m_[:], in0=m_[:], in1=lam, op=ALU.max))
        k.op(act, lambda: nc.scalar.activation(out=e_[:], in_=m_[:], func=AF.Exp, scale=-1.0),
             reads=[sw_b], writes=[sw_b])
        vop(lambda: nc.vector.tensor_scalar(out=z_[:], in0=e_[:], scalar1=2.0, scalar2=None, op0=ALU.add))
        vop(lambda: nc.vector.reciprocal(out=z_[:], in_=z_[:]))
        vop(lambda: nc.vector.tensor_tensor(out=z_[:], in0=z_[:], in1=e_[:], op=ALU.mult))
        vop(lambda: nc.vector.tensor_tensor(out=z2_[:], in0=z_[:], in1=z_[:], op=ALU.mult))
        vop(lambda: nc.vector.memset(p_, 1.0 / 17.0))
        for n in (15, 13, 11, 9, 7, 5, 3, 1):
            vop(lambda: nc.vector.tensor_tensor(out=p_, in0=p_, in1=z2_[:], op=ALU.mult))
            vop(lambda n=n: nc.vector.tensor_scalar(out=p_, in0=p_, scalar1=1.0 / n, scalar2=None, op0=ALU.add))
        vop(lambda: nc.vector.tensor_tensor(out=p_, in0=p_, in1=z_[:], op=ALU.mult))
        vop(lambda: nc.vector.tensor_scalar(out=m_[:], in0=lam, scalar1=-1.0, scalar2=0.0, op0=ALU.mult, op1=ALU.max))
        vop(lambda: nc.vector.scalar_tensor_tensor(out=m_[:], in0=p_, scalar=2.0, in1=m_[:], op0=ALU.mult, op1=ALU.add))
        vop(lambda: nc.vector.tensor_scalar(out=cv[:, 0, :], in0=m_[:], scalar1=-8.0, scalar2=None, op0=ALU.mult))
        vop(lambda: nc.vector.tensor_scalar(out=cv[:, 1, :], in0=m_[:], scalar1=-16.0, scalar2=None, op0=ALU.mult))

        for c in range(DC):
            k.dma(sp, d_p2, xr[:, 2:S + 2].rearrange("p (n s) -> p n s", n=NT),
                  XREC[:, :, c, :].rearrange("n p s -> p n s"), writes=[xr_b])
            def emit_conv(pc):
                o = pc * PT
                k.op(dve, lambda: nc.vector.tensor_scalar(
                    out=xc[:, o:o + PT], in0=xr[:, o:o + PT], scalar1=vecs[:, V_CW + c:V_CW + c + 1],
                    scalar2=vecs[:, V_CB + c:V_CB + c + 1], op0=ALU.mult, op1=ALU.add),
                    reads=[xr_b, vecs_b], writes=[xc_pb[pc]])
                for kk in range(1, 4):
                    k.op(dve, lambda kk=kk: nc.vector.scalar_tensor_tensor(
                        out=xc[:, o:o + PT], in0=xr[:, o + kk:o + kk + PT],
                        scalar=vecs[:, V_CW + kk * 16 + c:V_CW + kk * 16 + c + 1], in1=xc[:, o:o + PT],
                        op0=ALU.mult, op1=ALU.add),
                        reads=[xr_b, vecs_b, xc_pb[pc]], writes=[xc_pb[pc]])
                k.op(pool, lambda: nc.gpsimd.tensor_copy(out=xcb[:, o:o + PT], in_=xc[:, o:o + PT]),
                     reads=[xc_pb[pc]], writes=[xcb_pb[pc]])

            for pc in range(min(2, NP)):
                emit_conv(pc)
            for d in range(2):
                pieces = list(range(NP)) if d == 0 else list(range(NP - 1, -1, -1))
                for pi, pc in enumerate(pieces):
                    o = pc * PT
                    pp = ctr["pp"] % 2
                    ctr["pp"] += 1
                    rb_, rb_b, ib_, ib_b = rbL[pp], rbB[pp], ibL[pp], ibB[pp]
                    ab_, ab_b, ub_, ub_b = abL[pp], abB[pp], ubL[pp], ubB[pp]
                    hb_, hb_b, gg_, gg_b, hg_, hg_b = hbL[pp], hbB[pp], ggL[pp], ggB[pp], hgL[pp], hgB[pp]
                    pa0 = pp * 4
                    if d == 0 and pi + 2 < NP:
                        emit_conv(pi + 2)
                    for sub in range(NSUB):
                        rhs = xcb[:, o + sub * 512:o + (sub + 1) * 512]
                        mmgroup(pa0 + sub, [(rgw[:, 0, d, c, :], rhs, [rgw_b, xcb_pb[pc]])])
                        mmgroup(pa0 + 2 + sub, [(rgw[:, 1, d, c, :], rhs, [rgw_b, xcb_pb[pc]])])
                    bcol = d * 16 + c
                    for sub in range(NSUB):
                        k.op(act, lambda sub=sub: nc.scalar.activation(
                            out=rb_[:, sub * 512:(sub + 1) * 512], in_=PSt[pa0 + sub][:], func=AF.Sigmoid,
                            bias=vecs[:, V_BA + bcol:V_BA + bcol + 1], scale=1.0),
                            reads=[PSb[pa0 + sub], vecs_b], writes=[rb_b])
                    for sub in range(NSUB):
                        k.op(act, lambda sub=sub: nc.scalar.activation(
                            out=ib_[:, sub * 512:(sub + 1) * 512], in_=PSt[pa0 + 2 + sub][:], func=AF.Sigmoid,
                            bias=vecs[:, V_BI + bcol:V_BI + bcol + 1], scale=1.0),
                            reads=[PSb[pa0 + 2 + sub], vecs_b], writes=[ib_b])
                    k.op(act, lambda: nc.scalar.activation(out=ab_[:], in_=rb_[:], func=AF.Exp,
                                                           scale=cv[:, 0, bcol:bcol + 1]),
                         reads=[rb_b, cv_b], writes=[ab_b])
                    k.op(act, lambda: nc.scalar.activation(out=ub_[:], in_=rb_[:], func=AF.Exp,
                                                           scale=cv[:, 1, bcol:bcol + 1]),
                         reads=[rb_b, cv_b], writes=[ub_b])
                    k.op(act, lambda: nc.scalar.activation(out=ub_[:], in_=ub_[:], func=AF.Sqrt,
                                                           bias=onec[:, 0:1], scale=-1.0),
                         reads=[ub_b, onec_b], writes=[ub_b])
                    k.op(dve, lambda o=o: nc.vector.tensor_tensor(out=ib_[:], in0=ib_[:], in1=xc[:, o:o + PT],
                                                                  op=ALU.mult),
                         reads=[ib_b, xc_pb[pc]], writes=[ib_b])
                    k.op(dve, lambda: nc.vector.tensor_tensor(out=ub_[:], in0=ub_[:], in1=ib_[:], op=ALU.mult),
                         reads=[ub_b, ib_b], writes=[ub_b])
                    if d == 0:
                        init = 0.0 if pi == 0 else hf[:, o - 1:o]
                        k.op(dve, lambda o=o, init=init: nc.vector.tensor_tensor_scan(
                            out=hf[:, o:o + PT], data0=ab_[:], data1=ub_[:], initial=init,
                            op0=ALU.mult, op1=ALU.add),
                            reads=[ab_b, ub_b, hf_b], writes=[hf_b])
                    else:
                        def rev(tn, o0, n):
                            ap = tn[:, o0:o0 + n]
                            (ps_, pc_), (fs_, fc_) = ap.ap
                            return bass.AP(ap.tensor, ap.offset + (fc_ - 1) * fs_, [[ps_, pc_], [-fs_, fc_]])
                        init = 0.0 if pi == 0 else carry[:, 0:1]
                        k.op(dve, lambda init=init: nc.vector.tensor_tensor_scan(
                            out=rev(hb_, 0, PT), data0=rev(ab_, 0, PT), data1=rev(ub_, 0, PT), initial=init,
                            op0=ALU.mult, op1=ALU.add),
                            reads=[ab_b, ub_b, carry_b], writes=[hb_b])
                        k.op(dve, lambda: nc.vector.tensor_copy(out=carry[:], in_=hb_[:, 0:1]),
                             reads=[hb_b], writes=[carry_b])
                        k.dma(sp, d_gg[pp], gg_[:].rearrange("p (n s) -> p n s", n=PT // T),
                              GREC[pc * (PT // T):(pc + 1) * (PT // T), :, c, :].rearrange("n p s -> p n s"),
                              writes=[gg_b])
                        k.op(dve, lambda o=o: nc.vector.tensor_tensor(out=hb_[:], in0=hb_[:], in1=hf[:, o:o + PT],
                                                                      op=ALU.add),
                             reads=[hb_b, hf_b], writes=[hb_b])
                        k.op(dve, lambda: nc.vector.tensor_tensor(out=hg_[:], in0=hb_[:], in1=gg_[:], op=ALU.mult),
                             reads=[hb_b, gg_b], writes=[hg_b])
                        k.dma(pool, d_hg[pp],
                              HG[pc * (PT // T):(pc + 1) * (PT // T)].rearrange("n p (c s) -> p n c s", c=DC)[:, :, c, :],
                              hg_[:].rearrange("p (n s) -> p n s", n=PT // T), reads=[hg_b])
        k.barrier()
        es2.close()
        if stop == "p2":
            return nc

        es3 = contextlib.ExitStack()

        def sb3(name, shape, dt):
            return es3.enter_context(nc.sbuf_tensor(name, list(shape), dt))

        qn = sb3("qn", [128, NH, T], BF16)
        qn_b = k.buf()
        qpe = sb3("qpe", [128, NH, T], BF16)
        qpe_b = k.buf()
        kpe = sb3("kpe", [128, S], BF16)
        kpe_b = k.buf()
        Kb = [sb3("Kb%d" % i, [128, S], BF16) for i in range(2)]
        Kb_b = k.bufs_n(2)
        Vb = [sb3("Vb%d" % i, [128, NKC, 128], BF16) for i in range(2)]
        Vb_b = k.bufs_n(2)
        pT_ = [sb3("pT%d" % i, [128, T], BF16) for i in range(8)]
        pT_b = k.bufs_n(8)
        att = sb3("att", [128, NH, T], BF16)
        att_b = k.bufs_n(NH)
        rs = sb3("rs", [128, T], F32)
        rs_b = k.buf()
        accD = [[sb3("accD%d_%d" % (i, j), [128, T], F32) for j in range(2)] for i in range(2)]
        accD_b = [k.bufs_n(2) for i in range(2)]
        accP = [sb3("accP%d" % i, [128, T], F32) for i in range(2)]
        accP_b = k.bufs_n(2)
        d_q = k.dsem("q")
        d_kv = [k.dsem("kv%d" % i) for i in range(2)]
        d_att = k.dsem("att")
        k.dma(sp, d_q, kpe[0:64, :], KPE[:, :], writes=[kpe_b])
        k.dma(sp, d_q, kpe[64:128, :], KPE[:, :], writes=[kpe_b])
        LA = 2

        def load_kv(it):
            h = it % NH
            s = it % 2
            k.dma(sp, d_kv[s], Kb[s][:], KT[h], writes=[Kb_b[s]])
            k.dma(sp, d_kv[s], Vb[s][:].rearrange("p a e -> p (a e)"), VS[h], writes=[Vb_b[s]])

        load_kv(0)
        SB = [0, 1, 2, 5, 6, 7]
        sctr = 0
        pctr = 0
        NG = NKC // 4
        for t in range(NT):
            k.dma(sp, d_q, qn[:].rearrange("p h s -> p (h s)"), QN[t], writes=[qn_b])
            k.dma(sp, d_q, qpe[0:64, :, :].rearrange("p h s -> p (h s)"), QPE[t], writes=[qpe_b])
            k.dma(sp, d_q, qpe[64:128, :, :].rearrange("p h s -> p (h s)"), QPE[t], writes=[qpe_b])
            for h in range(NH):
                it = t * NH + h
                if it + 1 < NT * NH:
                    load_kv(it + 1)
                s = it % 2
                po = 3 + h % 2
                hp = h % 2
                nacc = 0
                npool = 0
                prev = None
                for g in range(NG + 1):
                    cur = None
                    if g < NG:
                        banks = [SB[(sctr + i) % 6] for i in range(4)]
                        sctr += 4
                        pts = [(pctr + i) % 8 for i in range(4)]
                        pctr += 4
                        kcs = [4 * g + i for i in range(4)]
                        for i in range(4):
                            ks = slice(kcs[i] * 128, (kcs[i] + 1) * 128)
                            k.op(pe, lambda i=i, ks=ks: nc.tensor.matmul(
                                PSt[banks[i]][:], Kb[s][:, ks], qn[:, h, :], start=True, stop=False),
                                reads=[Kb_b[s], qn_b], writes=[PSb[banks[i]]], inc=False)
                        for i in range(4):
                            ks = slice(kcs[i] * 128, (kcs[i] + 1) * 128)
                            r0 = 64 * (i % 2)
                            k.op(pe, lambda i=i, ks=ks, r0=r0: nc.tensor.matmul(
                                PSt[banks[i]][:], kpe[r0:r0 + 64, ks], qpe[r0:r0 + 64, h, :], start=False, stop=True),
                                reads=[kpe_b, qpe_b], writes=[PSb[banks[i]]])
                        for i in range(4):
                            pi = pts[i]
                            k.op(act, lambda i=i, pi=pi: nc.scalar.activation(
                                out=pT_[pi][:], in_=PSt[banks[i]][:], func=AF.Exp, scale=SCALE),
                                reads=[PSb[banks[i]]], writes=[pT_b[pi]])
                            if i == 3:
                                if npool == 0:
                                    k.op(pool, lambda pi=pi: nc.gpsimd.tensor_copy(out=accP[hp][:], in_=pT_[pi][:]),
                                         reads=[pT_b[pi]], writes=[accP_b[hp]])
                                else:
                                    k.op(pool, lambda pi=pi: nc.gpsimd.tensor_tensor(
                                        out=accP[hp][:], in0=accP[hp][:], in1=pT_[pi][:], op=ALU.add),
                                        reads=[pT_b[pi], accP_b[hp]], writes=[accP_b[hp]])
                                npool += 1
                            else:
                                ai = nacc % 2
                                if nacc < 2:
                                    k.op(dve, lambda pi=pi, ai=ai: nc.vector.tensor_copy(
                                        out=accD[hp][ai][:], in_=pT_[pi][:]),
                                        reads=[pT_b[pi]], writes=[accD_b[hp][ai]])
                                else:
                                    k.op(dve, lambda pi=pi, ai=ai: nc.vector.tensor_tensor(
                                        out=accD[hp][ai][:], in0=accD[hp][ai][:], in1=pT_[pi][:], op=ALU.add),
                                        reads=[pT_b[pi], accD_b[hp][ai]], writes=[accD_b[hp][ai]])
                                nacc += 1
                        cur = (kcs, pts)
                    if prev is not None:
                        pk, pp_ = prev
                        for i in range(4):
                            j = pk[i]
                            pj = pp_[i]
                            first, last = (j == 0), (j == NKC - 1)
                            k.op(pe, lambda j=j, pj=pj, first=first, last=last: nc.tensor.matmul(
                                PSt[po][:], Vb[s][:, j, :], pT_[pj][:], start=first, stop=last),
                                reads=[Vb_b[s], pT_b[pj]], writes=([PSb[po]] if (first or last) else ()), inc=last)
                    prev = cur
                psm = SB[sctr % 6]
                sctr += 1
                k.op(dve, lambda: nc.vector.tensor_tensor(out=accD[hp][0][:], in0=accD[hp][0][:], in1=accD[hp][1][:],
                                                          op=ALU.add),
                     reads=[accD_b[hp][0], accD_b[hp][1]], writes=[accD_b[hp][0]])
                k.op(dve, lambda: nc.vector.tensor_tensor(out=accD[hp][0][:], in0=accD[hp][0][:], in1=accP[hp][:],
                                                          op=ALU.add),
                     reads=[accD_b[hp][0], accP_b[hp]], writes=[accD_b[hp][0]])
                k.op(pe, lambda: nc.tensor.matmul(PSt[psm][:], ones_f[:], accD[hp][0][:], start=True, stop=True),
                     reads=[accD_b[hp][0], onesf_b], writes=[PSb[psm]])
                k.op(dve, lambda: nc.vector.reciprocal(out=rs[:], in_=PSt[psm][:]), reads=[PSb[psm]], writes=[rs_b])
                k.op(dve, lambda h=h: nc.vector.tensor_tensor(out=att[:, h, :], in0=PSt[po][:], in1=rs[:], op=ALU.mult),
                     reads=[PSb[po], rs_b], writes=[att_b[h]])
            k.dma(pool, d_att, ATT[t].rearrange("p (h s) -> p h s", h=NH), att[:], reads=att_b)
        k.barrier()
        es3.close()
        if stop == "p3a":
            return nc

        es1 = contextlib.ExitStack()
        xT = sb1("xTt2", [128, DC, T], F32)
        hT = sb1("hTt2", [128, DC, T], BF16)
        mid = sb1("midt2", [128, FC, T], BF16)
        ring = [sb1("ring2_%d" % i, [128, SLOT], BF16) for i in range(NSLOT)]
        sq = [sb1("sq2_%d" % i, [128, T], BF16) for i in range(2)]
        sd = sb1("sd2", [128, T], F32)
        rstd = sb1("rstd2", [128, T], F32)
        stg = [sb1("stg2_%d" % i, [128, T], F32) for i in range(4)]
        tmp = [sb1("tmp2_%d" % i, [128, T], F32) for i in range(4)]
        sgb = [sb1("sgb2_%d" % i, [128, T], F32) for i in range(4)]

        ws3 = WStream(k, ring, ring_b, ring_d)
        p3_idx = []
        for t in range(NT):
            ioa, ior = [None] * 4, [None] * 4
            for b in range(4):
                ioa[b] = ws3.add(s_woa[b], SLOT)
                ior[b] = ws3.add(s_wor[b], SLOT)
            iout = [ws3.add(s_wout[b], SLOT) for b in range(4)]
            ig = [None] * 11
            iu = [None] * 11
            for b in range(11):
                ig[b] = ws3.add(s_wg[1][b], SLOT)
                iu[b] = ws3.add(s_wu[1][b], SLOT)
            idn = [ws3.add(s_wd[1][b], FC * 128) for b in range(16)]
            p3_idx.append((ioa, ior, iout, ig, iu, idn))

        for t in range(NT):
            ioa, ior, iout, ig, iu, idn = p3_idx[t]
            tok = slice(t * T, (t + 1) * T)
            k.dma(pool, d_x, xT[:], X1T[t].rearrange("p (c s) -> p c s", c=DC), writes=xT_b)
            k.dma(pool, d_ld, mid[:, 0:16, :], ATT[t].rearrange("p (h s) -> p h s", h=NH), writes=mid_b[0:16])
            k.dma(pool, d_ld, mid[:, 16:32, :], HG[t].rearrange("p (c s) -> p c s", c=DC), writes=mid_b[16:32])
            for blk in range(4):
                wa, wab, wr, wrb = ws3.fetch(ioa[blk], 2)
                wa3 = wa[:].rearrange("p (c f) -> p c f", c=16, f=512)
                wr3 = wr[:].rearrange("p (c f) -> p c f", c=16, f=512)
                for j in range(4):
                    oc = blk * 4 + j
                    pa, pr = oc % 2, 2 + oc % 2
                    mmgroup(pa, [(wa3[:, c, j * 128:(j + 1) * 128], mid[:, c, :], [wab, mid_b[c]]) for c in range(16)])
                    mmgroup(pr, [(wr3[:, c, j * 128:(j + 1) * 128], mid[:, 16 + c, :], [wrb, mid_b[16 + c]])
                                 for c in range(16)])
                    sa, sr = (ctr["stg"]) % 4, (ctr["stg"] + 1) % 4
                    ctr["stg"] += 2
                    k.dma(pool, stg_d[sa], stg[sa][:], GATES[t][:, oc, :], writes=[stg_b[sa]])
                    k.dma(pool, stg_d[sr], stg[sr][:], GATES[t][:, 16 + oc, :], writes=[stg_b[sr]])
                    k.op(dve, lambda pa=pa, sa=sa: nc.vector.tensor_tensor(out=stg[sa][:], in0=PSt[pa][:], in1=stg[sa][:],
                                                                           op=ALU.mult),
                         reads=[PSb[pa], stg_b[sa]], writes=[stg_b[sa]])
                    k.op(dve, lambda pr=pr, sr=sr: nc.vector.tensor_tensor(out=stg[sr][:], in0=PSt[pr][:], in1=stg[sr][:],
                                                                           op=ALU.mult),
                         reads=[PSb[pr], stg_b[sr]], writes=[stg_b[sr]])
                    k.op(dve, lambda sa=sa, sr=sr, oc=oc: nc.vector.tensor_tensor(
                        out=hT[:, oc, :], in0=stg[sa][:], in1=stg[sr][:], op=ALU.add),
                        reads=[stg_b[sa], stg_b[sr]], writes=[hT_b[oc]])
            for blk in range(4):
                wo, wob = ws3.fetch(iout[blk])
                wo3 = wo[:].rearrange("p (c f) -> p c f", c=16, f=512)
                for j in range(4):
                    oc = blk * 4 + j
                    pd = 4 + oc % 2
                    mmgroup(pd, [(wo3[:, c, j * 128:(j + 1) * 128], hT[:, c, :], [wob, hT_b[c]]) for c in range(16)])
                    k.op(dve, lambda pd=pd, oc=oc: nc.vector.tensor_tensor(
                        out=xT[:, oc, :], in0=PSt[pd][:], in1=xT[:, oc, :], op=ALU.add),
                        reads=[PSb[pd], xT_b[oc]], writes=[xT_b[oc]])
            ffn(ws3, ig, iu, idn, V_G2)
            rmsnorm([xT[:, c, :] for c in range(DC)], xT_b, V_GF, float(D),
                    [xT[:, c, :] for c in range(DC)], xT_b, 6)
            k.dma(pool, d_x, yT_out.rearrange("(c p) s -> p c s", p=128)[:, :, tok], xT[:], reads=xT_b)
        k.barrier()
        es1.close()
    return nc


def _col(v):
    v = np.asarray(v, np.float32).reshape(-1, 128)
    return np.ascontiguousarray(v.T)


def host_inputs(inp, S):
    f = lambda a: np.ascontiguousarray(np.asarray(a, np.float32))
    w_in = f(inp["w_in"][0])
    w_uq = f(inp["w_uq"][0])
    kp = w_in[:, 1024:1088]
    w_kpesw = np.ascontiguousarray(np.concatenate([kp[:, 32:64], kp[:, 0:32]], axis=1))
    uq = w_uq.reshape(512, 16, 192)[:, :, 128:192]
    w_uqsw = np.ascontiguousarray(np.concatenate([uq[:, :, 32:64], uq[:, :, 0:32]], axis=2).reshape(512, 1024))
    vecs = np.zeros((128, NV), np.float32)
    vecs[:, V_G1:V_G1 + 16] = _col(inp["ffn1_norm"][0])
    vecs[:, V_GM:V_GM + 16] = _col(inp["mix_norm"][0])
    vecs[:, V_GQ:V_GQ + 4] = _col(inp["q_norm"][0])
    vecs[:, V_GKV:V_GKV + 4] = _col(inp["kv_norm"][0])
    vecs[:, V_G2:V_G2 + 16] = _col(inp["ffn2_norm"][0])
    vecs[:, V_GF:V_GF + 16] = _col(inp["final_norm"])
    for kk in range(4):
        vecs[:, V_CW + kk * 16:V_CW + kk * 16 + 16] = _col(inp["conv_w"][0][kk])
    vecs[:, V_CB:V_CB + 16] = _col(inp["conv_b"][0])
    for d in range(2):
        vecs[:, V_BA + d * 16:V_BA + d * 16 + 16] = _col(inp["rg_b_a"][0][d])
        vecs[:, V_BI + d * 16:V_BI + d * 16 + 16] = _col(inp["rg_b_i"][0][d])
        vecs[:, V_LAM + d * 16:V_LAM + d * 16 + 16] = _col(inp["rg_lambda"][0][d])
    pos = np.arange(S, dtype=np.float32)
    inv_freq = (np.float32(10000.0) ** (-np.arange(0, 64, 2, dtype=np.float32) / np.float32(64))).astype(np.float32)
    ang = (pos[:, None] * inv_freq[None, :]).astype(np.float32)
    cos = np.cos(ang).astype(np.float32).T
    sin = np.sin(ang).astype(np.float32).T
    ropeC = np.ascontiguousarray(np.concatenate([cos, cos], axis=0))
    ropeS = np.ascontiguousarray(np.concatenate([-sin, sin], axis=0))
    shared = {
        "ffn1_w_gate": f(inp["ffn1_w_gate"][0]), "ffn2_w_gate": f(inp["ffn2_w_gate"][0]),
        "ffn1_w_up": f(inp["ffn1_w_up"][0]), "ffn2_w_up": f(inp["ffn2_w_up"][0]),
        "ffn1_w_down": f(inp["ffn1_w_down"][0]), "ffn2_w_down": f(inp["ffn2_w_down"][0]),
        "w_in": w_in, "w_kpesw": w_kpesw, "w_uq": w_uq, "w_uqsw": w_uqsw, "w_ukv": f(inp["w_ukv"][0]),
        "w_o_attn": f(inp["w_o_attn"][0]), "w_o_rec": f(inp["w_o_rec"][0]), "w_out": f(inp["w_out"][0]),
        "rg_w_a": f(inp["rg_w_a"][0]), "rg_w_i": f(inp["rg_w_i"][0]),
        "vecs": vecs, "ropeC": ropeC, "ropeS": ropeS,
    }
    return shared


def kernel(**inputs):
    S = 8192
    xp = np.asarray(inputs["x_prompt"], np.float32)
    xs = np.asarray(inputs["x_sample"], np.float32)
    seqs = [xp[0]] + [xs[i] for i in range(4)]
    shared = host_inputs(inputs, S)
    nc = build(S)
    owner = [0, 1, 2, 4, 5]
    zero_x = np.zeros((D, S), np.float32)
    zero_shared = {name: np.zeros_like(a) for name, a in shared.items()}
    in_maps = []
    for c in range(8):
        if c in owner:
            m = dict(shared)
            m["xT"] = np.ascontiguousarray(seqs[owner.index(c)].T)
        else:
            m = dict(zero_shared)
            m["xT"] = zero_x
        in_maps.append(m)
    res = run_bass_kernel_spmd(nc, in_maps, core_ids=list(range(8)))
    outs = [np.ascontiguousarray(np.asarray(res.results[c]["yT"], np.float32).T) for c in owner]
    y_prompt = outs[0][None]
    y_sample = np.stack(outs[1:5], axis=0)
    return (y_prompt.astype(np.float32), y_sample.astype(np.float32))
```

```python
import contextlib
import numpy as np
import ml_dtypes
import concourse.bass as bass
import concourse.mybir as mybir
from concourse.bass_utils import run_bass_kernel_spmd

F32 = mybir.dt.float32
BF16 = mybir.dt.bfloat16
AF = mybir.ActivationFunctionType
ALU = mybir.AluOpType

D = 2048
DC = 16
FF = 5632
FC = 44
T = 512
NH = 16
EPS = 1e-6
N_IN = 9280
SCALE = 192.0 ** -0.5
NSLOT = 3
SLOT = 8192

V_G1, V_GM, V_GQ, V_GKV, V_G2, V_GF = 0, 16, 32, 36, 40, 56
V_CW, V_CB, V_BA, V_BI, V_LAM = 72, 136, 152, 184, 216
NV = 248


class DSem:
    def __init__(self, sem, key):
        self.sem, self.key, self.total = sem, key, 0


class Buf:
    __slots__ = ("lw", "rd", "dsem")

    def __init__(self):
        self.lw = None
        self.rd = {}
        self.dsem = None


class Eng:
    def __init__(self, name, eng, sem):
        self.name, self.eng, self.sem = name, eng, sem
        self.count = 0
        self.seen = {}

    def need(self, ev):
        if ev is None:
            return
        key, obj, v = ev
        if isinstance(obj, DSem):
            v = obj.total
            sem = obj.sem
        else:
            if key == "pe" and self.name == "pe":
                return
            sem = obj
        if self.seen.get(key, 0) >= v:
            return
        self.eng.wait_ge(sem, v)
        self.seen[key] = v


class K:
    def __init__(self, nc, es):
        self.nc = nc
        self.es = es
        self.pe = Eng("pe", nc.tensor, es.enter_context(nc.semaphore("s_pe")))
        self.act = Eng("act", nc.scalar, es.enter_context(nc.semaphore("s_act")))
        self.dve = Eng("dve", nc.vector, es.enter_context(nc.semaphore("s_dve")))
        self.pool = Eng("pool", nc.gpsimd, es.enter_context(nc.semaphore("s_pool")))
        self.sp = Eng("sp", nc.sync, None)
        self.engs = [self.pe, self.act, self.dve, self.pool, self.sp]
        self.dsems = []
        self.bufs = []

    def buf(self):
        b = Buf()
        self.bufs.append(b)
        return b

    def bufs_n(self, n):
        return [self.buf() for _ in range(n)]

    def dsem(self, name):
        d = DSem(self.es.enter_context(self.nc.semaphore("d_" + name)), "d_" + name + str(len(self.dsems)))
        self.dsems.append(d)
        return d

    def op(self, E, fn, reads=(), writes=(), inc=True):
        for b in reads:
            E.need(b.lw)
        for b in writes:
            E.need(b.lw)
            for ev in b.rd.values():
                E.need(ev)
        ins = fn()
        if inc:
            E.count += 1
            ins.then_inc(E.sem, 1)
            v = E.count
        else:
            v = E.count + 1
        ev = (E.name, E.sem, v)
        for b in reads:
            b.rd[E.name] = ev
        for b in writes:
            b.lw = ev
            b.rd = {}
        return ins

    def dma(self, Q, ds, out, in_, reads=(), writes=()):
        for b in reads:
            Q.need(b.lw)
        for b in writes:
            Q.need(b.lw)
            for ev in b.rd.values():
                Q.need(ev)
        ds.total += 16
        Q.eng.dma_start(out=out, in_=in_).then_inc(ds.sem, 16)
        ev = (ds.key, ds, ds.total)
        for b in reads:
            b.rd[ds.key] = ev
        for b in writes:
            b.lw = ev
            b.rd = {}

    def barrier(self):
        for F in self.engs:
            for E in self.engs:
                if E is F or E.sem is None:
                    continue
                if E.count > 0:
                    F.need((E.name, E.sem, E.count))
            for d in self.dsems:
                if d.total > 0:
                    F.need((d.key, d, d.total))
        for b in self.bufs:
            b.lw = None
            b.rd = {}


class WStream:
    def __init__(self, k, ring, ring_b, ring_d):
        self.k, self.ring, self.ring_b, self.ring_d = k, ring, ring_b, ring_d
        self.blocks = []
        self.issued = 0
        self.base = 0

    def add(self, ap2d, n):
        self.blocks.append((ap2d, n))
        return len(self.blocks) - 1

    def fetch(self, i, n=1):
        k = self.k
        assert i == self.base, (i, self.base)
        self.base = i + n
        hi = min(i + NSLOT - 1, len(self.blocks) - 1)
        while self.issued <= hi:
            j = self.issued
            ap2d, ne = self.blocks[j]
            s = j % NSLOT
            k.dma(k.sp, self.ring_d[s], self.ring[s][:, 0:ne], ap2d, writes=[self.ring_b[s]])
            self.issued += 1
        out = []
        for j in range(i, i + n):
            s = j % NSLOT
            out += [self.ring[s], self.ring_b[s]]
        return out


def build(S, debug=False, stop=None):
    NT = S // T
    NKC = S // 128
    PT = min(1024, S)
    NSUB = PT // 512
    NP = S // PT
    nc = bass.Bass("TRN2", target_bir_lowering=False)

    def din(name, shape, dt=F32):
        return nc.dram_tensor(name, list(shape), dt, kind="ExternalInput").ap()

    def dscr(name, shape, dt):
        kind = "ExternalOutput" if debug else "Internal"
        return nc.dram_tensor(name, list(shape), dt, kind=kind).ap()

    xT_in = din("xT", [D, S])
    w_g = [din("ffn1_w_gate", [D, FF]), din("ffn2_w_gate", [D, FF])]
    w_u = [din("ffn1_w_up", [D, FF]), din("ffn2_w_up", [D, FF])]
    w_d = [din("ffn1_w_down", [FF, D]), din("ffn2_w_down", [FF, D])]
    w_in = din("w_in", [D, N_IN])
    w_kpesw = din("w_kpesw", [D, 64])
    w_uq = din("w_uq", [512, 3072])
    w_uqsw = din("w_uqsw", [512, 1024])
    w_ukv = din("w_ukv", [512, 4096])
    w_oa = din("w_o_attn", [D, D])
    w_or = din("w_o_rec", [D, D])
    w_out = din("w_out", [D, D])
    rg_wa = din("rg_w_a", [2, 16, 128, 128])
    rg_wi = din("rg_w_i", [2, 16, 128, 128])
    vecs_in = din("vecs", [128, NV])
    ropeC_in = din("ropeC", [64, S])
    ropeS_in = din("ropeS", [64, S])
    yT_out = nc.dram_tensor("yT", [D, S], F32, kind="ExternalOutput").ap()

    s_wg = [dscr("s_wg%d" % i, [11, 128, SLOT], BF16) for i in range(2)]
    s_wu = [dscr("s_wu%d" % i, [11, 128, SLOT], BF16) for i in range(2)]
    s_wd = [dscr("s_wd%d" % i, [16, 128, FC * 128], BF16) for i in range(2)]
    s_win = dscr("s_win", [18, 128, SLOT], BF16)
    s_wkpe = dscr("s_wkpe", [128, 16 * 128], BF16)
    s_wuq = dscr("s_wuq", [2, 128, SLOT], BF16)
    s_wukv = dscr("s_wukv", [2, 128, SLOT], BF16)
    s_woa = dscr("s_woa", [4, 128, SLOT], BF16)
    s_wor = dscr("s_wor", [4, 128, SLOT], BF16)
    s_wout = dscr("s_wout", [4, 128, SLOT], BF16)
    X1T = dscr("X1T", [NT, 128, DC * T], F32)
    QN = dscr("QN", [NT, 128, NH * T], BF16)
    QPE = dscr("QPE", [NT, 64, NH * T], BF16)
    KT = dscr("KT", [NH, 128, S], BF16)
    VS = dscr("VS", [NH, 128, NKC * 128], BF16)
    KPE = dscr("KPE", [64, S], BF16)
    XREC = dscr("XREC", [NT, 128, DC, T], F32)
    GREC = dscr("GREC", [NT, 128, DC, T], F32)
    GATES = dscr("GATES", [NT, 128, 32, T], F32)
    HG = dscr("HG", [NT, 128, DC * T], BF16)
    ATT = dscr("ATT", [NT, 128, NH * T], BF16)

    es = contextlib.ExitStack()
    with es:
        k = K(nc, es)
        pe, act, dve, pool, sp = k.pe, k.act, k.dve, k.pool, k.sp

        def sb(name, shape, dt):
            return es.enter_context(nc.sbuf_tensor(name, list(shape), dt))

        PSt = [es.enter_context(nc.psum_tensor("ps%d" % i, [128, 512], F32)) for i in range(8)]
        PSb = k.bufs_n(8)

        vecs = sb("vecs_sb", [128, NV], F32)
        vecs_b = k.buf()
        ones = sb("ones", [128, 128], BF16)
        ones_b = k.buf()
        epsc = sb("epsc", [128, 1], F32)
        epsc_b = k.buf()
        d_misc = k.dsem("misc")
        k.dma(sp, d_misc, vecs[:], vecs_in[:, :], writes=[vecs_b])
        k.op(dve, lambda: nc.vector.memset(ones[:], 1.0), writes=[ones_b])
        k.op(dve, lambda: nc.vector.memset(epsc[:], EPS), writes=[epsc_b])
        ones_f = sb("ones_f", [128, 128], F32)
        onesf_b = k.buf()
        k.op(dve, lambda: nc.vector.memset(ones_f[:], 1.0), writes=[onesf_b])

        d_prep = k.dsem("prep")

        def prep(dst3, src3):
            k.dma(pool, d_prep, dst3, src3)

        def v3(ap2d, c, f):
            return ap2d.rearrange("p (c f) -> p c f", c=c, f=f)

        def prep_ffn(i):
            gv = w_g[i].rearrange("(c p) (b f) -> b p c f", p=128, f=512)
            uv = w_u[i].rearrange("(c p) (b f) -> b p c f", p=128, f=512)
            for b in range(11):
                prep(v3(s_wg[i][b], 16, 512), gv[b])
                prep(v3(s_wu[i][b], 16, 512), uv[b])
            dv = w_d[i].rearrange("(c p) (b f) -> b p c f", p=128, f=128)
            for b in range(16):
                prep(v3(s_wd[i][b], FC, 128), dv[b])

        prep_ffn(0)
        win_v = w_in.rearrange("(c p) n -> p c n", p=128)
        in_cols = [1088 + 512 * b for b in range(16)] + [0, 512]
        for b, c0 in enumerate(in_cols):
            prep(v3(s_win[b], 16, 512), win_v[:, :, c0:c0 + 512])
        kp3 = v3(s_wkpe, 16, 128)
        prep(kp3[:, :, 0:64], win_v[:, :, 1024:1088])
        prep(kp3[:, :, 64:128], w_kpesw.rearrange("(c p) n -> p c n", p=128))
        uq4 = w_uq.rearrange("(c p) (h e) -> p c h e", p=128, e=192)
        b0 = s_wuq[0].rearrange("p (c h e) -> p c h e", c=4, h=16, e=128)
        b1 = s_wuq[1].rearrange("p (c x) -> p c x", c=4, x=2048)
        for c4 in range(4):
            prep(b0[:, c4], uq4[:, c4, :, 0:128])
            prep(b1[:, c4, 0:1024].rearrange("p (h e) -> p h e", h=16, e=64), uq4[:, c4, :, 128:192])
        prep(b1[:, :, 1024:2048], w_uqsw.rearrange("(c p) n -> p c n", p=128))
        ukv4 = w_ukv.rearrange("(c p) (h e) -> p c h e", p=128, e=256)
        for c4 in range(4):
            prep(s_wukv[0].rearrange("p (c h e) -> p c h e", c=4, h=16, e=128)[:, c4], ukv4[:, c4, :, 0:128])
            prep(s_wukv[1].rearrange("p (c h e) -> p c h e", c=4, h=16, e=128)[:, c4], ukv4[:, c4, :, 128:256])
        k.barrier()

        def prep_group_b():
            for sw, w in ((s_woa, w_oa), (s_wor, w_or), (s_wout, w_out)):
                wv = w.rearrange("(c p) (b f) -> b p c f", p=128, f=512)
                for b in range(4):
                    prep(v3(sw[b], 16, 512), wv[b])
            prep_ffn(1)
        if stop == "p0":
            return nc

        es1 = contextlib.ExitStack()

        def sb1(name, shape, dt):
            return es1.enter_context(nc.sbuf_tensor(name, list(shape), dt))

        xT = sb1("xTt", [128, DC, T], F32)
        xT_b = k.bufs_n(DC)
        hT = sb1("hTt", [128, DC, T], BF16)
        hT_b = k.bufs_n(DC)
        mid = sb1("midt", [128, FC, T], BF16)
        mid_b = k.bufs_n(FC)
        ring = [sb1("ring%d" % i, [128, SLOT], BF16) for i in range(NSLOT)]
        ring_b = k.bufs_n(NSLOT)
        ring_d = [k.dsem("ring%d" % i) for i in range(NSLOT)]
        sq = [sb1("sq%d" % i, [128, T], BF16) for i in range(2)]
        sq_b = k.bufs_n(2)
        sd = sb1("sd", [128, T], F32)
        sd_b = k.buf()
        rstd = sb1("rstd", [128, T], F32)
        rstd_b = k.buf()
        stg = [sb1("stg%d" % i, [128, T], F32) for i in range(4)]
        stg_b = k.bufs_n(4)
        stg_d = [k.dsem("stg%d" % i) for i in range(4)]
        tmp = [sb1("tmp%d" % i, [128, T], F32) for i in range(4)]
        tmp_b = k.bufs_n(4)
        sgb = [sb1("sgb%d" % i, [128, T], F32) for i in range(4)]
        sgb_b = k.bufs_n(4)
        ropeC = sb1("ropeC_sb", [64, T], F32)
        ropeS = sb1("ropeS_sb", [64, T], F32)
        rope_b = k.buf()
        d_rope = k.dsem("rope")
        d_x = k.dsem("x")
        d_st = k.dsem("st")
        d_ld = k.dsem("ld")

        ctr = {"ps": 0, "stg": 0, "tmp": 0, "sq": 0, "pp": 0}

        def rmsnorm(srcs, src_bufs, gcol, dn, outs, out_bufs, ps_sum):
            C = len(srcs)
            for c in range(C):
                i = ctr["sq"] % 2
                ctr["sq"] += 1
                k.op(act, lambda c=c, i=i: nc.scalar.activation(out=sq[i][:], in_=srcs[c], func=AF.Square),
                     reads=[src_bufs[c]], writes=[sq_b[i]])
                k.op(pe, lambda c=c, i=i: nc.tensor.matmul(PSt[ps_sum][:], ones[:], sq[i][:], start=(c == 0),
                                                           stop=(c == C - 1)),
                     reads=[sq_b[i], ones_b], writes=([PSb[ps_sum]] if (c == 0 or c == C - 1) else ()))
            k.op(act, lambda: nc.scalar.activation(out=sd[:], in_=PSt[ps_sum][:], func=AF.Sqrt,
                                                   bias=epsc[:, 0:1], scale=1.0 / dn),
                 reads=[PSb[ps_sum], epsc_b], writes=[sd_b])
            k.op(dve, lambda: nc.vector.reciprocal(out=rstd[:], in_=sd[:]), reads=[sd_b], writes=[rstd_b])
            for c in range(C):
                k.op(dve, lambda c=c: nc.vector.scalar_tensor_tensor(
                    out=outs[c], in0=srcs[c], scalar=vecs[:, gcol + c:gcol + c + 1], in1=rstd[:],
                    op0=ALU.mult, op1=ALU.mult),
                    reads=[src_bufs[c], rstd_b, vecs_b], writes=[out_bufs[c]])

        def mmgroup(ps, items, m=128):
            n = len(items)
            for i, (l, r, rb) in enumerate(items):
                first, last = (i == 0), (i == n - 1)
                k.op(pe, lambda l=l, r=r, first=first, last=last: nc.tensor.matmul(
                    PSt[ps][0:m, :], l, r, start=first, stop=last),
                    reads=rb, writes=([PSb[ps]] if (first or last) else ()), inc=last)

        def ffn(ws, i_g, i_u, i_d, gcol):
            rmsnorm([xT[:, c, :] for c in range(DC)], xT_b, gcol, float(D),
                    [hT[:, c, :] for c in range(DC)], hT_b, 6)
            for fb in range(11):
                wg, wgb = ws.fetch(i_g[fb])
                wg3 = wg[:].rearrange("p (c f) -> p c f", c=16, f=512)
                for j in range(4):
                    fc = fb * 4 + j
                    pg = fc % 2
                    mmgroup(pg, [(wg3[:, c, j * 128:(j + 1) * 128], hT[:, c, :], [wgb, hT_b[c]]) for c in range(DC)])
                    k.op(act, lambda pg=pg, j=j: nc.scalar.activation(out=sgb[j][:], in_=PSt[pg][:], func=AF.Silu),
                         reads=[PSb[pg]], writes=[sgb_b[j]])
                wu, wub = ws.fetch(i_u[fb])
                wu3 = wu[:].rearrange("p (c f) -> p c f", c=16, f=512)
                for j in range(4):
                    fc = fb * 4 + j
                    pu = 2 + fc % 2
                    mmgroup(pu, [(wu3[:, c, j * 128:(j + 1) * 128], hT[:, c, :], [wub, hT_b[c]]) for c in range(DC)])
                    k.op(dve, lambda pu=pu, j=j, fc=fc: nc.vector.tensor_tensor(
                        out=mid[:, fc, :], in0=PSt[pu][:], in1=sgb[j][:], op=ALU.mult),
                        reads=[PSb[pu], sgb_b[j]], writes=[mid_b[fc]])
            for oc in range(DC):
                wd, wdb = ws.fetch(i_d[oc])
                wd3 = wd[:, 0:FC * 128].rearrange("p (c f) -> p c f", c=FC, f=128)
                pd = 4 + oc % 2
                mmgroup(pd, [(wd3[:, fc, :], mid[:, fc, :], [wdb, mid_b[fc]]) for fc in range(FC)])
                k.op(dve, lambda pd=pd, oc=oc: nc.vector.scalar_tensor_tensor(
                    out=xT[:, oc, :], in0=PSt[pd][:], scalar=0.5, in1=xT[:, oc, :], op0=ALU.mult, op1=ALU.add),
                    reads=[PSb[pd], xT_b[oc]], writes=[xT_b[oc]])

        def stage_store(dst_ap, fn_make, reads):
            i = ctr["stg"] % 4
            ctr["stg"] += 1
            fn_make(stg[i], stg_b[i], reads)
            k.dma(pool, stg_d[i], dst_ap, stg[i][:], reads=[stg_b[i]])

        ws1 = WStream(k, ring, ring_b, ring_d)
        p1_idx = []
        for t in range(NT):
            ig = [None] * 11
            iu = [None] * 11
            for b in range(11):
                ig[b] = ws1.add(s_wg[0][b], SLOT)
                iu[b] = ws1.add(s_wu[0][b], SLOT)
            idn = [ws1.add(s_wd[0][b], FC * 128) for b in range(16)]
            iin = [ws1.add(s_win[b], SLOT) for b in range(17)]
            iuq = [ws1.add(s_wuq[b], SLOT) for b in range(2)]
            iin.append(ws1.add(s_win[17], SLOT))
            ikpe = ws1.add(s_wkpe, 16 * 128)
            iukv = [ws1.add(s_wukv[b], SLOT) for b in range(2)]
            p1_idx.append((ig, iu, idn, iin, ikpe, iuq, iukv))

        for t in range(NT):
            ig, iu, idn, iin, ikpe, iuq, iukv = p1_idx[t]
            tok = slice(t * T, (t + 1) * T)
            k.dma(pool, d_x, xT[:], xT_in.rearrange("(c p) s -> p c s", p=128)[:, :, tok], writes=xT_b)
            k.dma(pool, d_rope, ropeC[:], ropeC_in[:, tok], writes=[rope_b])
            k.dma(pool, d_rope, ropeS[:], ropeS_in[:, tok], writes=[rope_b])
            if t == 0:
                prep_group_b()
            ffn(ws1, ig, iu, idn, V_G1)
            k.dma(pool, d_x, X1T[t].rearrange("p (c s) -> p c s", c=DC), xT[:], reads=xT_b)
            rmsnorm([xT[:, c, :] for c in range(DC)], xT_b, V_GM, float(D),
                    [hT[:, c, :] for c in range(DC)], hT_b, 6)

            def inproj_chunk(w3, wb, j, ps, m=128):
                mmgroup(ps, [(w3[:, c, j * m:(j + 1) * m], hT[:, c, :], [wb, hT_b[c]]) for c in range(DC)], m=m)

            for blk in range(16):
                w, wb = ws1.fetch(iin[blk])
                w3 = w[:].rearrange("p (c f) -> p c f", c=16, f=512)
                for j in range(4):
                    ps = ctr["ps"] % 4
                    ctr["ps"] += 1
                    inproj_chunk(w3, wb, j, ps)
                    ch = (blk % 4) * 4 + j if blk < 8 else (blk - 8) * 4 + j
                    if blk < 4:
                        def mk(st, stb, rd, ps=ps):
                            k.op(act, lambda: nc.scalar.activation(out=st[:], in_=PSt[ps][:], func=AF.Copy),
                                 reads=[PSb[ps]], writes=[stb])
                        stage_store(XREC[t][:, ch, :], mk, None)
                    elif blk < 8:
                        def mk(st, stb, rd, ps=ps):
                            ti = ctr["tmp"] % 4
                            ctr["tmp"] += 1
                            k.op(act, lambda: nc.scalar.activation(out=tmp[ti][:], in_=PSt[ps][:], func=AF.Square),
                                 reads=[PSb[ps]], writes=[tmp_b[ti]])
                            k.op(dve, lambda: nc.vector.tensor_scalar(
                                out=tmp[ti][:], in0=tmp[ti][:], scalar1=0.044715, scalar2=1.0,
                                op0=ALU.mult, op1=ALU.add), reads=[tmp_b[ti]], writes=[tmp_b[ti]])
                            k.op(dve, lambda: nc.vector.tensor_tensor(
                                out=tmp[ti][:], in0=PSt[ps][:], in1=tmp[ti][:], op=ALU.mult),
                                reads=[PSb[ps], tmp_b[ti]], writes=[tmp_b[ti]])
                            k.op(act, lambda: nc.scalar.activation(out=tmp[ti][:], in_=tmp[ti][:], func=AF.Sigmoid,
                                                                   scale=1.5957691216057308),
                                 reads=[tmp_b[ti]], writes=[tmp_b[ti]])
                            k.op(dve, lambda: nc.vector.tensor_tensor(
                                out=st[:], in0=PSt[ps][:], in1=tmp[ti][:], op=ALU.mult),
                                reads=[PSb[ps], tmp_b[ti]], writes=[stb])
                        stage_store(GREC[t][:, ch, :], mk, None)
                    else:
                        def mk(st, stb, rd, ps=ps):
                            k.op(act, lambda: nc.scalar.activation(out=st[:], in_=PSt[ps][:], func=AF.Sigmoid),
                                 reads=[PSb[ps]], writes=[stb])
                        stage_store(GATES[t][:, ch, :], mk, None)

            w, wb = ws1.fetch(iin[16])
            w3 = w[:].rearrange("p (c f) -> p c f", c=16, f=512)
            for j in range(4):
                inproj_chunk(w3, wb, j, j)
            cqn = [mid[:, 36 + c, :] for c in range(4)]
            cqn_b = [mid_b[36 + c] for c in range(4)]
            rmsnorm([PSt[j][:] for j in range(4)], [PSb[j] for j in range(4)], V_GQ, 512.0, cqn, cqn_b, 6)
            wq0, wq0b, wq1, wq1b = ws1.fetch(iuq[0], 2)
            wq0_3 = wq0[:].rearrange("p (c x) -> p c x", c=4, x=2048)
            wq1_3 = wq1[:].rearrange("p (c x) -> p c x", c=4, x=2048)
            for h in range(NH):
                ps = 4 + h % 2
                mmgroup(ps, [(wq0_3[:, c, h * 128:(h + 1) * 128], cqn[c], [wq0b, cqn_b[c]]) for c in range(4)])
                k.op(act, lambda ps=ps, h=h: nc.scalar.activation(out=mid[:, h, :], in_=PSt[ps][:], func=AF.Copy),
                     reads=[PSb[ps]], writes=[mid_b[h]])
                pa, pb = 0 + 2 * (h % 2), 1 + 2 * (h % 2)
                mmgroup(pa, [(wq1_3[:, c, h * 64:(h + 1) * 64], cqn[c], [wq1b, cqn_b[c]]) for c in range(4)], m=64)
                mmgroup(pb, [(wq1_3[:, c, 1024 + h * 64:1024 + (h + 1) * 64], cqn[c], [wq1b, cqn_b[c]])
                             for c in range(4)], m=64)
                t0, t1 = (ctr["tmp"]) % 4, (ctr["tmp"] + 1) % 4
                ctr["tmp"] += 2
                k.op(dve, lambda pa=pa, t0=t0: nc.vector.tensor_tensor(
                    out=tmp[t0][0:64, :], in0=PSt[pa][0:64, :], in1=ropeC[:], op=ALU.mult),
                    reads=[PSb[pa], rope_b], writes=[tmp_b[t0]])
                k.op(dve, lambda pb=pb, t1=t1: nc.vector.tensor_tensor(
                    out=tmp[t1][0:64, :], in0=PSt[pb][0:64, :], in1=ropeS[:], op=ALU.mult),
                    reads=[PSb[pb], rope_b], writes=[tmp_b[t1]])
                k.op(dve, lambda t0=t0, t1=t1, h=h: nc.vector.tensor_tensor(
                    out=mid[0:64, 16 + h, :], in0=tmp[t0][0:64, :], in1=tmp[t1][0:64, :], op=ALU.add),
                    reads=[tmp_b[t0], tmp_b[t1]], writes=[mid_b[16 + h]])
            k.dma(pool, d_st, QN[t].rearrange("p (h s) -> p h s", h=NH), mid[:, 0:16, :], reads=mid_b[0:16])
            k.dma(pool, d_st, QPE[t].rearrange("p (h s) -> p h s", h=NH), mid[0:64, 16:32, :], reads=mid_b[16:32])

            w, wb = ws1.fetch(iin[17])
            w3 = w[:].rearrange("p (c f) -> p c f", c=16, f=512)
            for j in range(4):
                inproj_chunk(w3, wb, j, j)
            ckn = [mid[:, 40 + c, :] for c in range(4)]
            ckn_b = [mid_b[40 + c] for c in range(4)]
            rmsnorm([PSt[j][:] for j in range(4)], [PSb[j] for j in range(4)], V_GKV, 512.0, ckn, ckn_b, 6)
            wkp, wkpb = ws1.fetch(ikpe)
            wkp3 = wkp[:, 0:2048].rearrange("p (c f) -> p c f", c=16, f=128)
            mmgroup(0, [(wkp3[:, c, 0:64], hT[:, c, :], [wkpb, hT_b[c]]) for c in range(DC)], m=64)
            mmgroup(1, [(wkp3[:, c, 64:128], hT[:, c, :], [wkpb, hT_b[c]]) for c in range(DC)], m=64)
            t0, t1 = (ctr["tmp"]) % 4, (ctr["tmp"] + 1) % 4
            ctr["tmp"] += 2
            k.op(dve, lambda: nc.vector.tensor_tensor(out=tmp[t0][0:64, :], in0=PSt[0][0:64, :], in1=ropeC[:],
                                                      op=ALU.mult), reads=[PSb[0], rope_b], writes=[tmp_b[t0]])
            k.op(dve, lambda: nc.vector.tensor_tensor(out=tmp[t1][0:64, :], in0=PSt[1][0:64, :], in1=ropeS[:],
                                                      op=ALU.mult), reads=[PSb[1], rope_b], writes=[tmp_b[t1]])
            k.op(dve, lambda: nc.vector.tensor_tensor(out=mid[0:64, 35, :], in0=tmp[t0][0:64, :],
                                                      in1=tmp[t1][0:64, :], op=ALU.add),
                 reads=[tmp_b[t0], tmp_b[t1]], writes=[mid_b[35]])
            k.dma(pool, d_st, KPE[:, tok], mid[0:64, 35, :], reads=[mid_b[35]])
            wk, wkb = ws1.fetch(iukv[0])
            wk3 = wk[:].rearrange("p (c x) -> p c x", c=4, x=2048)
            for h in range(NH):
                ps = 4 + h % 2
                mmgroup(ps, [(wk3[:, c, h * 128:(h + 1) * 128], ckn[c], [wkb, ckn_b[c]]) for c in range(4)])
                k.op(act, lambda ps=ps, h=h: nc.scalar.activation(out=mid[:, h, :], in_=PSt[ps][:], func=AF.Copy),
                     reads=[PSb[ps]], writes=[mid_b[h]])
            k.dma(pool, d_st, KT.rearrange("h p s -> p h s")[:, :, tok], mid[:, 0:16, :], reads=mid_b[0:16])
            wv, wvb = ws1.fetch(iukv[1])
            wv3 = wv[:].rearrange("p (c x) -> p c x", c=4, x=2048)
            for tc in range(4):
                for g in range(4):
                    ps = (tc * 4 + g) % 2 + 2
                    mmgroup(ps, [(ckn[c][:, tc * 128:(tc + 1) * 128], wv3[:, c, g * 512:(g + 1) * 512],
                                  [wvb, ckn_b[c]]) for c in range(4)])
                    k.op(dve, lambda ps=ps, tc=tc, g=g: nc.vector.tensor_copy(
                        out=mid[:, 16 + tc * 4 + g, :], in_=PSt[ps][:]),
                        reads=[PSb[ps]], writes=[mid_b[16 + tc * 4 + g]])
                k.dma(pool, d_st,
                      VS.rearrange("h p (kc e) -> p kc h e", e=128)[:, t * 4 + tc, :, :],
                      mid[:, 16 + tc * 4:20 + tc * 4, :].rearrange("p g (hh e) -> p (g hh) e", e=128),
                      reads=mid_b[16 + tc * 4:20 + tc * 4])
        k.barrier()
        es1.close()
        if stop == "p1":
            return nc

        es2 = contextlib.ExitStack()

        def sb2(name, shape, dt):
            return es2.enter_context(nc.sbuf_tensor(name, list(shape), dt))

        xr = sb2("xr", [128, S + 3], F32)
        xr_b = k.buf()
        xc = sb2("xc", [128, S], F32)
        xc_b = k.buf()
        xcb = sb2("xcb", [128, S], BF16)
        xcb_b = k.buf()
        xc_pb = k.bufs_n(NP)
        xcb_pb = k.bufs_n(NP)
        hf = sb2("hf", [128, S], F32)
        hf_b = k.buf()
        rgw = sb2("rgw", [128, 2, 2, 16, 128], BF16)
        rgw_b = k.buf()
        rbL = [sb2("rbuf%d" % i, [128, PT], F32) for i in range(2)]
        rbB = k.bufs_n(2)
        ibL = [sb2("ibuf%d" % i, [128, PT], F32) for i in range(2)]
        ibB = k.bufs_n(2)
        abL = [sb2("abuf%d" % i, [128, PT], F32) for i in range(2)]
        abB = k.bufs_n(2)
        ubL = [sb2("ubuf%d" % i, [128, PT], F32) for i in range(2)]
        ubB = k.bufs_n(2)
        hbL = [sb2("hbuf%d" % i, [128, PT], F32) for i in range(2)]
        hbB = k.bufs_n(2)
        ggL = [sb2("ggbuf%d" % i, [128, PT], F32) for i in range(2)]
        ggB = k.bufs_n(2)
        hgL = [sb2("hgbuf%d" % i, [128, PT], BF16) for i in range(2)]
        hgB = k.bufs_n(2)
        onec = sb2("onec", [128, 1], F32)
        onec_b = k.buf()
        k.op(dve, lambda: nc.vector.memset(onec[:], 1.0), writes=[onec_b])
        d_gg = [k.dsem("gg%d" % i) for i in range(2)]
        d_hg = [k.dsem("hg%d" % i) for i in range(2)]
        carry = sb2("carry", [128, 1], F32)
        carry_b = k.buf()
        cv = sb2("cv", [128, 3, 32], F32)
        cv_b = k.buf()
        sw_ = [sb2("swk%d" % i, [128, 32], F32) for i in range(4)]
        sw_b = k.buf()
        d_p2 = k.dsem("p2")
        d_p2s = k.dsem("p2s")
        d_rgw = k.dsem("rgw")

        for dd in range(2):
            k.dma(pool, d_rgw, rgw[:, 0, dd, :, :], rg_wa[dd].rearrange("n k j -> k n j"), writes=[rgw_b])
            k.dma(pool, d_rgw, rgw[:, 1, dd, :, :], rg_wi[dd].rearrange("n k j -> k n j"), writes=[rgw_b])
        k.op(pool, lambda: nc.gpsimd.memset(xr[:, 0:2], 0.0), writes=[xr_b])
        k.op(pool, lambda: nc.gpsimd.memset(xr[:, S + 2:S + 3], 0.0), writes=[xr_b])
        lam = vecs[:, V_LAM:V_LAM + 32]
        m_, e_, z_, z2_ = sw_[0], sw_[1], sw_[2], sw_[3]
        p_ = cv[:, 2, :]

        def vop(fn):
            k.op(dve, fn, reads=[vecs_b, sw_b, cv_b], writes=[sw_b, cv_b])
        vop(lambda: nc.vector.tensor_scalar(out=m_[:], in0=lam, scalar1=-1.0, scalar2=None, op0=ALU.mult))
        vop(lambda: nc.vector.tensor_tensor(out=m_[:], in0=m_[:], in1=lam, op=ALU.max))
        k.op(act, lambda: nc.scalar.activation(out=e_[:], in_=m_[:], func=AF.Exp, scale=-1.0),
             reads=[sw_b], writes=[sw_b])
        vop(lambda: nc.vector.tensor_scalar(out=z_[:], in0=e_[:], scalar1=2.0, scalar2=None, op0=ALU.add))
        vop(lambda: nc.vector.reciprocal(out=z_[:], in_=z_[:]))
        vop(lambda: nc.vector.tensor_tensor(out=z_[:], in0=z_[:], in1=e_[:], op=ALU.mult))
        vop(lambda: nc.vector.tensor_tensor(out=z2_[:], in0=z_[:], in1=z_[:], op=ALU.mult))
        vop(lambda: nc.vector.memset(p_, 1.0 / 17.0))
        for n in (15, 13, 11, 9, 7, 5, 3, 1):
            vop(lambda: nc.vector.tensor_tensor(out=p_, in0=p_, in1=z2_[:], op=ALU.mult))
            vop(lambda n=n: nc.vector.tensor_scalar(out=p_, in0=p_, scalar1=1.0 / n, scalar2=None, op0=ALU.add))
        vop(lambda: nc.vector.tensor_tensor(out=p_, in0=p_, in1=z_[:], op=ALU.mult))
        vop(lambda: nc.vector.tensor_scalar(out=m_[:], in0=lam, scalar1=-1.0, scalar2=0.0, op0=ALU.mult, op1=ALU.max))
        vop(lambda: nc.vector.scalar_tensor_tensor(out=m_[:], in0=p_, scalar=2.0, in1=m_[:], op0=ALU.mult, op1=ALU.add))
        vop(lambda: nc.vector.tensor_scalar(out=cv[:, 0, :], in0=m_[:], scalar1=-8.0, scalar2=None, op0=ALU.mult))
        vop(lambda: nc.vector.tensor_scalar(out=cv[:, 1, :], in0=m_[:], scalar1=-16.0, scalar2=None, op0=ALU.mult))

        for c in range(DC):
            k.dma(sp, d_p2, xr[:, 2:S + 2].rearrange("p (n s) -> p n s", n=NT),
                  XREC[:, :, c, :].rearrange("n p s -> p n s"), writes=[xr_b])
            def emit_conv(pc):
                o = pc * PT
                k.op(dve, lambda: nc.vector.tensor_scalar(
                    out=xc[:, o:o + PT], in0=xr[:, o:o + PT], scalar1=vecs[:, V_CW + c:V_CW + c + 1],
                    scalar2=vecs[:, V_CB + c:V_CB + c + 1], op0=ALU.mult, op1=ALU.add),
                    reads=[xr_b, vecs_b], writes=[xc_pb[pc]])
                for kk in range(1, 4):
                    k.op(dve, lambda kk=kk: nc.vector.scalar_tensor_tensor(
                        out=xc[:, o:o + PT], in0=xr[:, o + kk:o + kk + PT],
                        scalar=vecs[:, V_CW + kk * 16 + c:V_CW + kk * 16 + c + 1], in1=xc[:, o:o + PT],
                        op0=ALU.mult, op1=ALU.add),
                        reads=[xr_b, vecs_b, xc_pb[pc]], writes=[xc_pb[pc]])
                k.op(pool, lambda: nc.gpsimd.tensor_copy(out=xcb[:, o:o + PT], in_=xc[:, o:o + PT]),
                     reads=[xc_pb[pc]], writes=[xcb_pb[pc]])

            for pc in range(min(2, NP)):
                emit_conv(pc)
            for d in range(2):
                pieces = list(range(NP)) if d == 0 else list(range(NP - 1, -1, -1))
                for pi, pc in enumerate(pieces):
                    o = pc * PT
                    pp = ctr["pp"] % 2
                    ctr["pp"] += 1
                    rb_, rb_b, ib_, ib_b = rbL[pp], rbB[pp], ibL[pp], ibB[pp]
                    ab_, ab_b, ub_, ub_b = abL[pp], abB[pp], ubL[pp], ubB[pp]
                    hb_, hb_b, gg_, gg_b, hg_, hg_b = hbL[pp], hbB[pp], ggL[pp], ggB[pp], hgL[pp], hgB[pp]
                    pa0 = pp * 4
                    if d == 0 and pi + 2 < NP:
                        emit_conv(pi + 2)
                    for sub in range(NSUB):
                        rhs = xcb[:, o + sub * 512:o + (sub + 1) * 512]
                        mmgroup(pa0 + sub, [(rgw[:, 0, d, c, :], rhs, [rgw_b, xcb_pb[pc]])])
                        mmgroup(pa0 + 2 + sub, [(rgw[:, 1, d, c, :], rhs, [rgw_b, xcb_pb[pc]])])
                    bcol = d * 16 + c
                    for sub in range(NSUB):
                        k.op(act, lambda sub=sub: nc.scalar.activation(
                            out=rb_[:, sub * 512:(sub + 1) * 512], in_=PSt[pa0 + sub][:], func=AF.Sigmoid,
                            bias=vecs[:, V_BA + bcol:V_BA + bcol + 1], scale=1.0),
                            reads=[PSb[pa0 + sub], vecs_b], writes=[rb_b])
                    for sub in range(NSUB):
                        k.op(act, lambda sub=sub: nc.scalar.activation(
                            out=ib_[:, sub * 512:(sub + 1) * 512], in_=PSt[pa0 + 2 + sub][:], func=AF.Sigmoid,
                            bias=vecs[:, V_BI + bcol:V_BI + bcol + 1], scale=1.0),
                            reads=[PSb[pa0 + 2 + sub], vecs_b], writes=[ib_b])
                    k.op(act, lambda: nc.scalar.activation(out=ab_[:], in_=rb_[:], func=AF.Exp,
                                                           scale=cv[:, 0, bcol:bcol + 1]),
                         reads=[rb_b, cv_b], writes=[ab_b])
                    k.op(act, lambda: nc.scalar.activation(out=ub_[:], in_=rb_[:], func=AF.Exp,
                                                           scale=cv[:, 1, bcol:bcol + 1]),
                         reads=[rb_b, cv_b], writes=[ub_b])
                    k.op(act, lambda: nc.scalar.activation(out=ub_[:], in_=ub_[:], func=AF.Sqrt,
                                                           bias=onec[:, 0:1], scale=-1.0),
                         reads=[ub_b, onec_b], writes=[ub_b])
                    k.op(dve, lambda o=o: nc.vector.tensor_tensor(out=ib_[:], in0=ib_[:], in1=xc[:, o:o + PT],
                                                                  op=ALU.mult),
                         reads=[ib_b, xc_pb[pc]], writes=[ib_b])
                    k.op(dve, lambda: nc.vector.tensor_tensor(out=ub_[:], in0=ub_[:], in1=ib_[:], op=ALU.mult),
                         reads=[ub_b, ib_b], writes=[ub_b])
                    if d == 0:
                        init = 0.0 if pi == 0 else hf[:, o - 1:o]
                        k.op(dve, lambda o=o, init=init: nc.vector.tensor_tensor_scan(
                            out=hf[:, o:o + PT], data0=ab_[:], data1=ub_[:], initial=init,
                            op0=ALU.mult, op1=ALU.add),
                            reads=[ab_b, ub_b, hf_b], writes=[hf_b])
                    else:
                        def rev(tn, o0, n):
                            ap = tn[:, o0:o0 + n]
                            (ps_, pc_), (fs_, fc_) = ap.ap
                            return bass.AP(ap.tensor, ap.offset + (fc_ - 1) * fs_, [[ps_, pc_], [-fs_, fc_]])
                        init = 0.0 if pi == 0 else carry[:, 0:1]
                        k.op(dve, lambda init=init: nc.vector.tensor_tensor_scan(
                            out=rev(hb_, 0, PT), data0=rev(ab_, 0, PT), data1=rev(ub_, 0, PT), initial=init,
                            op0=ALU.mult, op1=ALU.add),
                            reads=[ab_b, ub_b, carry_b], writes=[hb_b])
                        k.op(dve, lambda: nc.vector.tensor_copy(out=carry[:], in_=hb_[:, 0:1]),
                             reads=[hb_b], writes=[carry_b])
                        k.dma(sp, d_gg[pp], gg_[:].rearrange("p (n s) -> p n s", n=PT // T),
                              GREC[pc * (PT // T):(pc + 1) * (PT // T), :, c, :].rearrange("n p s -> p n s"),
                              writes=[gg_b])
                        k.op(dve, lambda o=o: nc.vector.tensor_tensor(out=hb_[:], in0=hb_[:], in1=hf[:, o:o + PT],
                                                                      op=ALU.add),
                             reads=[hb_b, hf_b], writes=[hb_b])
                        k.op(dve, lambda: nc.vector.tensor_tensor(out=hg_[:], in0=hb_[:], in1=gg_[:], op=ALU.mult),
                             reads=[hb_b, gg_b], writes=[hg_b])
                        k.dma(pool, d_hg[pp],
                              HG[pc * (PT // T):(pc + 1) * (PT // T)].rearrange("n p (c s) -> p n c s", c=DC)[:, :, c, :],
                              hg_[:].rearrange("p (n s) -> p n s", n=PT // T), reads=[hg_b])
        k.barrier()
        es2.close()
        if stop == "p2":
            return nc

        es3 = contextlib.ExitStack()

        def sb3(name, shape, dt):
            return es3.enter_context(nc.sbuf_tensor(name, list(shape), dt))

        qn = sb3("qn", [128, NH, T], BF16)
        qn_b = k.buf()
        qpe = sb3("qpe", [128, NH, T], BF16)
        qpe_b = k.buf()
        kpe = sb3("kpe", [128, S], BF16)
        kpe_b = k.buf()
        Kb = [sb3("Kb%d" % i, [128, S], BF16) for i in range(2)]
        Kb_b = k.bufs_n(2)
        Vb = [sb3("Vb%d" % i, [128, NKC, 128], BF16) for i in range(2)]
        Vb_b = k.bufs_n(2)
        pT_ = [sb3("pT%d" % i, [128, T], BF16) for i in range(8)]
        pT_b = k.bufs_n(8)
        att = sb3("att", [128, NH, T], BF16)
        att_b = k.bufs_n(NH)
        rs = sb3("rs", [128, T], F32)
        rs_b = k.buf()
        accD = [[sb3("accD%d_%d" % (i, j), [128, T], F32) for j in range(2)] for i in range(2)]
        accD_b = [k.bufs_n(2) for i in range(2)]
        accP = [sb3("accP%d" % i, [128, T], F32) for i in range(2)]
        accP_b = k.bufs_n(2)
        d_q = k.dsem("q")
        d_kv = [k.dsem("kv%d" % i) for i in range(2)]
        d_att = k.dsem("att")
        k.dma(sp, d_q, kpe[0:64, :], KPE[:, :], writes=[kpe_b])
        k.dma(sp, d_q, kpe[64:128, :], KPE[:, :], writes=[kpe_b])
        LA = 2

        def load_kv(it):
            h = it % NH
            s = it % 2
            k.dma(sp, d_kv[s], Kb[s][:], KT[h], writes=[Kb_b[s]])
            k.dma(sp, d_kv[s], Vb[s][:].rearrange("p a e -> p (a e)"), VS[h], writes=[Vb_b[s]])

        load_kv(0)
        SB = [0, 1, 2, 5, 6, 7]
        sctr = 0
        pctr = 0
        NG = NKC // 4
        for t in range(NT):
            k.dma(sp, d_q, qn[:].rearrange("p h s -> p (h s)"), QN[t], writes=[qn_b])
            k.dma(sp, d_q, qpe[0:64, :, :].rearrange("p h s -> p (h s)"), QPE[t], writes=[qpe_b])
            k.dma(sp, d_q, qpe[64:128, :, :].rearrange("p h s -> p (h s)"), QPE[t], writes=[qpe_b])
            for h in range(NH):
                it = t * NH + h
                if it + 1 < NT * NH:
                    load_kv(it + 1)
                s = it % 2
                po = 3 + h % 2
                hp = h % 2
                nacc = 0
                npool = 0
                prev = None
                for g in range(NG + 1):
                    cur = None
                    if g < NG:
                        banks = [SB[(sctr + i) % 6] for i in range(4)]
                        sctr += 4
                        pts = [(pctr + i) % 8 for i in range(4)]
                        pctr += 4
                        kcs = [4 * g + i for i in range(4)]
                        for i in range(4):
                            ks = slice(kcs[i] * 128, (kcs[i] + 1) * 128)
                            k.op(pe, lambda i=i, ks=ks: nc.tensor.matmul(
                                PSt[banks[i]][:], Kb[s][:, ks], qn[:, h, :], start=True, stop=False),
                                reads=[Kb_b[s], qn_b], writes=[PSb[banks[i]]], inc=False)
                        for i in range(4):
                            ks = slice(kcs[i] * 128, (kcs[i] + 1) * 128)
                            r0 = 64 * (i % 2)
                            k.op(pe, lambda i=i, ks=ks, r0=r0: nc.tensor.matmul(
                                PSt[banks[i]][:], kpe[r0:r0 + 64, ks], qpe[r0:r0 + 64, h, :], start=False, stop=True),
                                reads=[kpe_b, qpe_b], writes=[PSb[banks[i]]])
                        for i in range(4):
                            pi = pts[i]
                            k.op(act, lambda i=i, pi=pi: nc.scalar.activation(
                                out=pT_[pi][:], in_=PSt[banks[i]][:], func=AF.Exp, scale=SCALE),
                                reads=[PSb[banks[i]]], writes=[pT_b[pi]])
                            if i == 3 or (i == 1 and g % 2 == 1):
                                if npool == 0:
                                    k.op(pool, lambda pi=pi: nc.gpsimd.tensor_copy(out=accP[hp][:], in_=pT_[pi][:]),
                                         reads=[pT_b[pi]], writes=[accP_b[hp]])
                                else:
                                    k.op(pool, lambda pi=pi: nc.gpsimd.tensor_tensor(
                                        out=accP[hp][:], in0=accP[hp][:], in1=pT_[pi][:], op=ALU.add),
                                        reads=[pT_b[pi], accP_b[hp]], writes=[accP_b[hp]])
                                npool += 1
                            else:
                                ai = nacc % 2
                                if nacc < 2:
                                    k.op(dve, lambda pi=pi, ai=ai: nc.vector.tensor_copy(
                                        out=accD[hp][ai][:], in_=pT_[pi][:]),
                                        reads=[pT_b[pi]], writes=[accD_b[hp][ai]])
                                else:
                                    k.op(dve, lambda pi=pi, ai=ai: nc.vector.tensor_tensor(
                                        out=accD[hp][ai][:], in0=accD[hp][ai][:], in1=pT_[pi][:], op=ALU.add),
                                        reads=[pT_b[pi], accD_b[hp][ai]], writes=[accD_b[hp][ai]])
                                nacc += 1
                        cur = (kcs, pts)
                    if prev is not None:
                        pk, pp_ = prev
                        for i in range(4):
                            j = pk[i]
                            pj = pp_[i]
                            first, last = (j == 0), (j == NKC - 1)
                            k.op(pe, lambda j=j, pj=pj, first=first, last=last: nc.tensor.matmul(
                                PSt[po][:], Vb[s][:, j, :], pT_[pj][:], start=first, stop=last),
                                reads=[Vb_b[s], pT_b[pj]], writes=([PSb[po]] if (first or last) else ()), inc=last)
                    prev = cur
                psm = SB[sctr % 6]
                sctr += 1
                k.op(dve, lambda: nc.vector.tensor_tensor(out=accD[hp][0][:], in0=accD[hp][0][:], in1=accD[hp][1][:],
                                                          op=ALU.add),
                     reads=[accD_b[hp][0], accD_b[hp][1]], writes=[accD_b[hp][0]])
                k.op(dve, lambda: nc.vector.tensor_tensor(out=accD[hp][0][:], in0=accD[hp][0][:], in1=accP[hp][:],
                                                          op=ALU.add),
                     reads=[accD_b[hp][0], accP_b[hp]], writes=[accD_b[hp][0]])
                k.op(pe, lambda: nc.tensor.matmul(PSt[psm][:], ones_f[:], accD[hp][0][:], start=True, stop=True),
                     reads=[accD_b[hp][0], onesf_b], writes=[PSb[psm]])
                k.op(dve, lambda: nc.vector.reciprocal(out=rs[:], in_=PSt[psm][:]), reads=[PSb[psm]], writes=[rs_b])
                k.op(dve, lambda h=h: nc.vector.tensor_tensor(out=att[:, h, :], in0=PSt[po][:], in1=rs[:], op=ALU.mult),
                     reads=[PSb[po], rs_b], writes=[att_b[h]])
            k.dma(pool, d_att, ATT[t].rearrange("p (h s) -> p h s", h=NH), att[:], reads=att_b)
        k.barrier()
        es3.close()
        if stop == "p3a":
            return nc

        es1 = contextlib.ExitStack()
        xT = sb1("xTt2", [128, DC, T], F32)
        hT = sb1("hTt2", [128, DC, T], BF16)
        mid = sb1("midt2", [128, FC, T], BF16)
        ring = [sb1("ring2_%d" % i, [128, SLOT], BF16) for i in range(NSLOT)]
        sq = [sb1("sq2_%d" % i, [128, T], BF16) for i in range(2)]
        sd = sb1("sd2", [128, T], F32)
        rstd = sb1("rstd2", [128, T], F32)
        stg = [sb1("stg2_%d" % i, [128, T], F32) for i in range(4)]
        tmp = [sb1("tmp2_%d" % i, [128, T], F32) for i in range(4)]
        sgb = [sb1("sgb2_%d" % i, [128, T], F32) for i in range(4)]

        ws3 = WStream(k, ring, ring_b, ring_d)
        p3_idx = []
        for t in range(NT):
            ioa, ior = [None] * 4, [None] * 4
            for b in range(4):
                ioa[b] = ws3.add(s_woa[b], SLOT)
                ior[b] = ws3.add(s_wor[b], SLOT)
            iout = [ws3.add(s_wout[b], SLOT) for b in range(4)]
            ig = [None] * 11
            iu = [None] * 11
            for b in range(11):
                ig[b] = ws3.add(s_wg[1][b], SLOT)
                iu[b] = ws3.add(s_wu[1][b], SLOT)
            idn = [ws3.add(s_wd[1][b], FC * 128) for b in range(16)]
            p3_idx.append((ioa, ior, iout, ig, iu, idn))

        for t in range(NT):
            ioa, ior, iout, ig, iu, idn = p3_idx[t]
            tok = slice(t * T, (t + 1) * T)
            k.dma(pool, d_x, xT[:], X1T[t].rearrange("p (c s) -> p c s", c=DC), writes=xT_b)
            k.dma(pool, d_ld, mid[:, 0:16, :], ATT[t].rearrange("p (h s) -> p h s", h=NH), writes=mid_b[0:16])
            k.dma(pool, d_ld, mid[:, 16:32, :], HG[t].rearrange("p (c s) -> p c s", c=DC), writes=mid_b[16:32])
            for blk in range(4):
                wa, wab, wr, wrb = ws3.fetch(ioa[blk], 2)
                wa3 = wa[:].rearrange("p (c f) -> p c f", c=16, f=512)
                wr3 = wr[:].rearrange("p (c f) -> p c f", c=16, f=512)
                for j in range(4):
                    oc = blk * 4 + j
                    pa, pr = oc % 2, 2 + oc % 2
                    mmgroup(pa, [(wa3[:, c, j * 128:(j + 1) * 128], mid[:, c, :], [wab, mid_b[c]]) for c in range(16)])
                    mmgroup(pr, [(wr3[:, c, j * 128:(j + 1) * 128], mid[:, 16 + c, :], [wrb, mid_b[16 + c]])
                                 for c in range(16)])
                    sa, sr = (ctr["stg"]) % 4, (ctr["stg"] + 1) % 4
                    ctr["stg"] += 2
                    k.dma(pool, stg_d[sa], stg[sa][:], GATES[t][:, oc, :], writes=[stg_b[sa]])
                    k.dma(pool, stg_d[sr], stg[sr][:], GATES[t][:, 16 + oc, :], writes=[stg_b[sr]])
                    k.op(dve, lambda pa=pa, sa=sa: nc.vector.tensor_tensor(out=stg[sa][:], in0=PSt[pa][:], in1=stg[sa][:],
                                                                           op=ALU.mult),
                         reads=[PSb[pa], stg_b[sa]], writes=[stg_b[sa]])
                    k.op(dve, lambda pr=pr, sr=sr: nc.vector.tensor_tensor(out=stg[sr][:], in0=PSt[pr][:], in1=stg[sr][:],
                                                                           op=ALU.mult),
                         reads=[PSb[pr], stg_b[sr]], writes=[stg_b[sr]])
                    k.op(dve, lambda sa=sa, sr=sr, oc=oc: nc.vector.tensor_tensor(
                        out=hT[:, oc, :], in0=stg[sa][:], in1=stg[sr][:], op=ALU.add),
                        reads=[stg_b[sa], stg_b[sr]], writes=[hT_b[oc]])
            for blk in range(4):
                wo, wob = ws3.fetch(iout[blk])
                wo3 = wo[:].rearrange("p (c f) -> p c f", c=16, f=512)
                for j in range(4):
                    oc = blk * 4 + j
                    pd = 4 + oc % 2
                    mmgroup(pd, [(wo3[:, c, j * 128:(j + 1) * 128], hT[:, c, :], [wob, hT_b[c]]) for c in range(16)])
                    k.op(dve, lambda pd=pd, oc=oc: nc.vector.tensor_tensor(
                        out=xT[:, oc, :], in0=PSt[pd][:], in1=xT[:, oc, :], op=ALU.add),
                        reads=[PSb[pd], xT_b[oc]], writes=[xT_b[oc]])
            ffn(ws3, ig, iu, idn, V_G2)
            rmsnorm([xT[:, c, :] for c in range(DC)], xT_b, V_GF, float(D),
                    [xT[:, c, :] for c in range(DC)], xT_b, 6)
            k.dma(pool, d_x, yT_out.rearrange("(c p) s -> p c s", p=128)[:, :, tok], xT[:], reads=xT_b)
        k.barrier()
        es1.close()
    return nc


def _col(v):
    v = np.asarray(v, np.float32).reshape(-1, 128)
    return np.ascontiguousarray(v.T)


def host_inputs(inp, S):
    f = lambda a: np.ascontiguousarray(np.asarray(a, np.float32))
    w_in = f(inp["w_in"][0])
    w_uq = f(inp["w_uq"][0])
    kp = w_in[:, 1024:1088]
    w_kpesw = np.ascontiguousarray(np.concatenate([kp[:, 32:64], kp[:, 0:32]], axis=1))
    uq = w_uq.reshape(512, 16, 192)[:, :, 128:192]
    w_uqsw = np.ascontiguousarray(np.concatenate([uq[:, :, 32:64], uq[:, :, 0:32]], axis=2).reshape(512, 1024))
    vecs = np.zeros((128, NV), np.float32)
    vecs[:, V_G1:V_G1 + 16] = _col(inp["ffn1_norm"][0])
    vecs[:, V_GM:V_GM + 16] = _col(inp["mix_norm"][0])
    vecs[:, V_GQ:V_GQ + 4] = _col(inp["q_norm"][0])
    vecs[:, V_GKV:V_GKV + 4] = _col(inp["kv_norm"][0])
    vecs[:, V_G2:V_G2 + 16] = _col(inp["ffn2_norm"][0])
    vecs[:, V_GF:V_GF + 16] = _col(inp["final_norm"])
    for kk in range(4):
        vecs[:, V_CW + kk * 16:V_CW + kk * 16 + 16] = _col(inp["conv_w"][0][kk])
    vecs[:, V_CB:V_CB + 16] = _col(inp["conv_b"][0])
    for d in range(2):
        vecs[:, V_BA + d * 16:V_BA + d * 16 + 16] = _col(inp["rg_b_a"][0][d])
        vecs[:, V_BI + d * 16:V_BI + d * 16 + 16] = _col(inp["rg_b_i"][0][d])
        vecs[:, V_LAM + d * 16:V_LAM + d * 16 + 16] = _col(inp["rg_lambda"][0][d])
    pos = np.arange(S, dtype=np.float32)
    inv_freq = (np.float32(10000.0) ** (-np.arange(0, 64, 2, dtype=np.float32) / np.float32(64))).astype(np.float32)
    ang = (pos[:, None] * inv_freq[None, :]).astype(np.float32)
    cos = np.cos(ang).astype(np.float32).T
    sin = np.sin(ang).astype(np.float32).T
    ropeC = np.ascontiguousarray(np.concatenate([cos, cos], axis=0))
    ropeS = np.ascontiguousarray(np.concatenate([-sin, sin], axis=0))
    shared = {
        "ffn1_w_gate": f(inp["ffn1_w_gate"][0]), "ffn2_w_gate": f(inp["ffn2_w_gate"][0]),
        "ffn1_w_up": f(inp["ffn1_w_up"][0]), "ffn2_w_up": f(inp["ffn2_w_up"][0]),
        "ffn1_w_down": f(inp["ffn1_w_down"][0]), "ffn2_w_down": f(inp["ffn2_w_down"][0]),
        "w_in": w_in, "w_kpesw": w_kpesw, "w_uq": w_uq, "w_uqsw": w_uqsw, "w_ukv": f(inp["w_ukv"][0]),
        "w_o_attn": f(inp["w_o_attn"][0]), "w_o_rec": f(inp["w_o_rec"][0]), "w_out": f(inp["w_out"][0]),
        "rg_w_a": f(inp["rg_w_a"][0]), "rg_w_i": f(inp["rg_w_i"][0]),
        "vecs": vecs, "ropeC": ropeC, "ropeS": ropeS,
    }
    return shared


def kernel(**inputs):
    S = 8192
    xp = np.asarray(inputs["x_prompt"], np.float32)
    xs = np.asarray(inputs["x_sample"], np.float32)
    seqs = [xp[0]] + [xs[i] for i in range(4)]
    shared = host_inputs(inputs, S)
    nc = build(S)
    owner = [0, 1, 2, 4, 5]
    zero_x = np.zeros((D, S), np.float32)
    zero_shared = {name: np.zeros_like(a) for name, a in shared.items()}
    in_maps = []
    for c in range(8):
        if c in owner:
            m = dict(shared)
            m["xT"] = np.ascontiguousarray(seqs[owner.index(c)].T)
        else:
            m = dict(zero_shared)
            m["xT"] = zero_x
        in_maps.append(m)
    res = run_bass_kernel_spmd(nc, in_maps, core_ids=list(range(8)))
    outs = [np.ascontiguousarray(np.asarray(res.results[c]["yT"], np.float32).T) for c in owner]
    y_prompt = outs[0][None]
    y_sample = np.stack(outs[1:5], axis=0)
    return (y_prompt.astype(np.float32), y_sample.astype(np.float32))
```

```python
import contextlib
import numpy as np
import ml_dtypes
import concourse.bass as bass
import concourse.mybir as mybir
from concourse.bass_utils import run_bass_kernel_spmd

F32 = mybir.dt.float32
BF16 = mybir.dt.bfloat16
AF = mybir.ActivationFunctionType
ALU = mybir.AluOpType

D = 2048
DC = 16
FF = 5632
FC = 44
T = 512
NH = 16
EPS = 1e-6
N_IN = 9280
SCALE = 192.0 ** -0.5
NSLOT = 3
SLOT = 8192

V_G1, V_GM, V_GQ, V_GKV, V_G2, V_GF = 0, 16, 32, 36, 40, 56
V_CW, V_CB, V_BA, V_BI, V_LAM = 72, 136, 152, 184, 216
NV = 248


class DSem:
    def __init__(self, sem, key):
        self.sem, self.key, self.total = sem, key, 0


class Buf:
    __slots__ = ("lw", "rd", "dsem")

    def __init__(self):
        self.lw = None
        self.rd = {}
        self.dsem = None


class Eng:
    def __init__(self, name, eng, sem):
        self.name, self.eng, self.sem = name, eng, sem
        self.count = 0
        self.seen = {}

    def need(self, ev):
        if ev is None:
            return
        key, obj, v = ev
        if isinstance(obj, DSem):
            v = obj.total
            sem = obj.sem
        else:
            if key == "pe" and self.name == "pe":
                return
            sem = obj
        if self.seen.get(key, 0) >= v:
            return
        self.eng.wait_ge(sem, v)
        self.seen[key] = v


class K:
    def __init__(self, nc, es):
        self.nc = nc
        self.es = es
        self.pe = Eng("pe", nc.tensor, es.enter_context(nc.semaphore("s_pe")))
        self.act = Eng("act", nc.scalar, es.enter_context(nc.semaphore("s_act")))
        self.dve = Eng("dve", nc.vector, es.enter_context(nc.semaphore("s_dve")))
        self.pool = Eng("pool", nc.gpsimd, es.enter_context(nc.semaphore("s_pool")))
        self.sp = Eng("sp", nc.sync, None)
        self.engs = [self.pe, self.act, self.dve, self.pool, self.sp]
        self.dsems = []
        self.bufs = []

    def buf(self):
        b = Buf()
        self.bufs.append(b)
        return b

    def bufs_n(self, n):
        return [self.buf() for _ in range(n)]

    def dsem(self, name):
        d = DSem(self.es.enter_context(self.nc.semaphore("d_" + name)), "d_" + name + str(len(self.dsems)))
        self.dsems.append(d)
        return d

    def op(self, E, fn, reads=(), writes=(), inc=True):
        for b in reads:
            E.need(b.lw)
        for b in writes:
            E.need(b.lw)
            for ev in b.rd.values():
                E.need(ev)
        ins = fn()
        if inc:
            E.count += 1
            ins.then_inc(E.sem, 1)
            v = E.count
        else:
            v = E.count + 1
        ev = (E.name, E.sem, v)
        for b in reads:
            b.rd[E.name] = ev
        for b in writes:
            b.lw = ev
            b.rd = {}
        return ins

    def dma(self, Q, ds, out, in_, reads=(), writes=()):
        for b in reads:
            Q.need(b.lw)
        for b in writes:
            Q.need(b.lw)
            for ev in b.rd.values():
                Q.need(ev)
        ds.total += 16
        Q.eng.dma_start(out=out, in_=in_).then_inc(ds.sem, 16)
        ev = (ds.key, ds, ds.total)
        for b in reads:
            b.rd[ds.key] = ev
        for b in writes:
            b.lw = ev
            b.rd = {}

    def barrier(self):
        for F in self.engs:
            for E in self.engs:
                if E is F or E.sem is None:
                    continue
                if E.count > 0:
                    F.need((E.name, E.sem, E.count))
            for d in self.dsems:
                if d.total > 0:
                    F.need((d.key, d, d.total))
        for b in self.bufs:
            b.lw = None
            b.rd = {}


class WStream:
    def __init__(self, k, ring, ring_b, ring_d):
        self.k, self.ring, self.ring_b, self.ring_d = k, ring, ring_b, ring_d
        self.blocks = []
        self.issued = 0
        self.base = 0
        self.gate = None
        self.gate_from = 22

    def add(self, ap2d, n):
        self.blocks.append((ap2d, n))
        return len(self.blocks) - 1

    def fetch(self, i, n=1):
        k = self.k
        assert i == self.base, (i, self.base)
        self.base = i + n
        hi = min(i + NSLOT - 1, len(self.blocks) - 1)
        while self.issued <= hi:
            j = self.issued
            ap2d, ne = self.blocks[j]
            s = j % NSLOT
            if self.gate is not None and j >= self.gate_from:
                k.sp.need(self.gate)
            k.dma(k.sp, self.ring_d[s], self.ring[s][:, 0:ne], ap2d, writes=[self.ring_b[s]])
            self.issued += 1
        out = []
        for j in range(i, i + n):
            s = j % NSLOT
            out += [self.ring[s], self.ring_b[s]]
        return out


def build(S, debug=False, stop=None):
    NT = S // T
    NKC = S // 128
    PT = min(1024, S)
    NSUB = PT // 512
    NP = S // PT
    nc = bass.Bass("TRN2", target_bir_lowering=False)

    def din(name, shape, dt=F32):
        return nc.dram_tensor(name, list(shape), dt, kind="ExternalInput").ap()

    def dscr(name, shape, dt):
        kind = "ExternalOutput" if debug else "Internal"
        return nc.dram_tensor(name, list(shape), dt, kind=kind).ap()

    xT_in = din("xT", [D, S])
    w_g = [din("ffn1_w_gate", [D, FF]), din("ffn2_w_gate", [D, FF])]
    w_u = [din("ffn1_w_up", [D, FF]), din("ffn2_w_up", [D, FF])]
    w_d = [din("ffn1_w_down", [FF, D]), din("ffn2_w_down", [FF, D])]
    w_in = din("w_in", [D, N_IN])
    w_kpesw = din("w_kpesw", [D, 64])
    w_uq = din("w_uq", [512, 3072])
    w_uqsw = din("w_uqsw", [512, 1024])
    w_ukv = din("w_ukv", [512, 4096])
    w_oa = din("w_o_attn", [D, D])
    w_or = din("w_o_rec", [D, D])
    w_out = din("w_out", [D, D])
    rg_wa = din("rg_w_a", [2, 16, 128, 128])
    rg_wi = din("rg_w_i", [2, 16, 128, 128])
    vecs_in = din("vecs", [128, NV])
    ropeC_in = din("ropeC", [64, S])
    ropeS_in = din("ropeS", [64, S])
    yT_out = nc.dram_tensor("yT", [D, S], F32, kind="ExternalOutput").ap()

    s_wg = [dscr("s_wg%d" % i, [11, 128, SLOT], BF16) for i in range(2)]
    s_wu = [dscr("s_wu%d" % i, [11, 128, SLOT], BF16) for i in range(2)]
    s_wd = [dscr("s_wd%d" % i, [16, 128, FC * 128], BF16) for i in range(2)]
    s_win = dscr("s_win", [18, 128, SLOT], BF16)
    s_wkpe = dscr("s_wkpe", [128, 16 * 128], BF16)
    s_wuq = dscr("s_wuq", [2, 128, SLOT], BF16)
    s_wukv = dscr("s_wukv", [2, 128, SLOT], BF16)
    s_woa = dscr("s_woa", [4, 128, SLOT], BF16)
    s_wor = dscr("s_wor", [4, 128, SLOT], BF16)
    s_wout = dscr("s_wout", [4, 128, SLOT], BF16)
    X1T = dscr("X1T", [NT, 128, DC * T], F32)
    QN = dscr("QN", [NT, 128, NH * T], BF16)
    QPE = dscr("QPE", [NT, 64, NH * T], BF16)
    KT = dscr("KT", [NH, 128, S], BF16)
    VS = dscr("VS", [NH, 128, NKC * 128], BF16)
    KPE = dscr("KPE", [64, S], BF16)
    XREC = dscr("XREC", [NT, 128, DC, T], F32)
    GREC = dscr("GREC", [NT, 128, DC, T], F32)
    GATES = dscr("GATES", [NT, 128, 32, T], F32)
    HG = dscr("HG", [NT, 128, DC * T], BF16)
    ATT = dscr("ATT", [NT, 128, NH * T], BF16)

    es = contextlib.ExitStack()
    with es:
        k = K(nc, es)
        pe, act, dve, pool, sp = k.pe, k.act, k.dve, k.pool, k.sp

        def sb(name, shape, dt):
            return es.enter_context(nc.sbuf_tensor(name, list(shape), dt))

        PSt = [es.enter_context(nc.psum_tensor("ps%d" % i, [128, 512], F32)) for i in range(8)]
        PSb = k.bufs_n(8)

        vecs = sb("vecs_sb", [128, NV], F32)
        vecs_b = k.buf()
        ones = sb("ones", [128, 128], BF16)
        ones_b = k.buf()
        epsc = sb("epsc", [128, 1], F32)
        epsc_b = k.buf()
        d_misc = k.dsem("misc")
        k.dma(sp, d_misc, vecs[:], vecs_in[:, :], writes=[vecs_b])
        k.op(dve, lambda: nc.vector.memset(ones[:], 1.0), writes=[ones_b])
        k.op(dve, lambda: nc.vector.memset(epsc[:], EPS), writes=[epsc_b])
        ones_f = sb("ones_f", [128, 128], F32)
        onesf_b = k.buf()
        k.op(dve, lambda: nc.vector.memset(ones_f[:], 1.0), writes=[onesf_b])

        d_prep = k.dsem("prep")

        d_prep2 = k.dsem("prep2")
        cur_prep = [d_prep]

        def prep(dst3, src3):
            k.dma(pool, cur_prep[0], dst3, src3)

        def v3(ap2d, c, f):
            return ap2d.rearrange("p (c f) -> p c f", c=c, f=f)

        def prep_ffn_gu(i):
            gv = w_g[i].rearrange("(c p) (b f) -> b p c f", p=128, f=512)
            uv = w_u[i].rearrange("(c p) (b f) -> b p c f", p=128, f=512)
            for b in range(11):
                prep(v3(s_wg[i][b], 16, 512), gv[b])
                prep(v3(s_wu[i][b], 16, 512), uv[b])

        def prep_ffn_d(i):
            dv = w_d[i].rearrange("(c p) (b f) -> b p c f", p=128, f=128)
            for b in range(16):
                prep(v3(s_wd[i][b], FC, 128), dv[b])

        def prep_ffn(i):
            prep_ffn_gu(i)
            prep_ffn_d(i)

        prep_ffn_gu(0)
        k.barrier()

        def prep_rest_a():
            cur_prep[0] = d_prep2
            prep_ffn_d(0)
            prep_rest_a_body()
            cur_prep[0] = d_prep

        def prep_rest_a_body():
            win_v = w_in.rearrange("(c p) n -> p c n", p=128)
            in_cols = [1088 + 512 * b for b in range(16)] + [0, 512]
            for b, c0 in enumerate(in_cols):
                prep(v3(s_win[b], 16, 512), win_v[:, :, c0:c0 + 512])
            kp3 = v3(s_wkpe, 16, 128)
            prep(kp3[:, :, 0:64], win_v[:, :, 1024:1088])
            prep(kp3[:, :, 64:128], w_kpesw.rearrange("(c p) n -> p c n", p=128))
            uq4 = w_uq.rearrange("(c p) (h e) -> p c h e", p=128, e=192)
            b0 = s_wuq[0].rearrange("p (c h e) -> p c h e", c=4, h=16, e=128)
            b1 = s_wuq[1].rearrange("p (c x) -> p c x", c=4, x=2048)
            for c4 in range(4):
                prep(b0[:, c4], uq4[:, c4, :, 0:128])
                prep(b1[:, c4, 0:1024].rearrange("p (h e) -> p h e", h=16, e=64), uq4[:, c4, :, 128:192])
            prep(b1[:, :, 1024:2048], w_uqsw.rearrange("(c p) n -> p c n", p=128))
            ukv4 = w_ukv.rearrange("(c p) (h e) -> p c h e", p=128, e=256)
            for c4 in range(4):
                prep(s_wukv[0].rearrange("p (c h e) -> p c h e", c=4, h=16, e=128)[:, c4], ukv4[:, c4, :, 0:128])
                prep(s_wukv[1].rearrange("p (c h e) -> p c h e", c=4, h=16, e=128)[:, c4], ukv4[:, c4, :, 128:256])


        def prep_group_b():
            for sw, w in ((s_woa, w_oa), (s_wor, w_or), (s_wout, w_out)):
                wv = w.rearrange("(c p) (b f) -> b p c f", p=128, f=512)
                for b in range(4):
                    prep(v3(sw[b], 16, 512), wv[b])
            prep_ffn(1)
        if stop == "p0":
            return nc

        es1 = contextlib.ExitStack()

        def sb1(name, shape, dt):
            return es1.enter_context(nc.sbuf_tensor(name, list(shape), dt))

        xT = sb1("xTt", [128, DC, T], F32)
        xT_b = k.bufs_n(DC)
        hT = sb1("hTt", [128, DC, T], BF16)
        hT_b = k.bufs_n(DC)
        mid = sb1("midt", [128, FC, T], BF16)
        mid_b = k.bufs_n(FC)
        ring = [sb1("ring%d" % i, [128, SLOT], BF16) for i in range(NSLOT)]
        ring_b = k.bufs_n(NSLOT)
        ring_d = [k.dsem("ring%d" % i) for i in range(NSLOT)]
        sq = [sb1("sq%d" % i, [128, T], BF16) for i in range(2)]
        sq_b = k.bufs_n(2)
        sd = sb1("sd", [128, T], F32)
        sd_b = k.buf()
        rstd = sb1("rstd", [128, T], F32)
        rstd_b = k.buf()
        stg = [sb1("stg%d" % i, [128, T], F32) for i in range(4)]
        stg_b = k.bufs_n(4)
        stg_d = [k.dsem("stg%d" % i) for i in range(4)]
        tmp = [sb1("tmp%d" % i, [128, T], F32) for i in range(4)]
        tmp_b = k.bufs_n(4)
        sgb = [sb1("sgb%d" % i, [128, T], F32) for i in range(4)]
        sgb_b = k.bufs_n(4)
        ropeC = sb1("ropeC_sb", [64, T], F32)
        ropeS = sb1("ropeS_sb", [64, T], F32)
        rope_b = k.buf()
        d_rope = k.dsem("rope")
        d_x = k.dsem("x")
        d_st = k.dsem("st")
        d_ld = k.dsem("ld")

        ctr = {"ps": 0, "stg": 0, "tmp": 0, "sq": 0, "pp": 0}

        def rmsnorm(srcs, src_bufs, gcol, dn, outs, out_bufs, ps_sum):
            C = len(srcs)
            for c in range(C):
                i = ctr["sq"] % 2
                ctr["sq"] += 1
                k.op(act, lambda c=c, i=i: nc.scalar.activation(out=sq[i][:], in_=srcs[c], func=AF.Square),
                     reads=[src_bufs[c]], writes=[sq_b[i]])
                k.op(pe, lambda c=c, i=i: nc.tensor.matmul(PSt[ps_sum][:], ones[:], sq[i][:], start=(c == 0),
                                                           stop=(c == C - 1)),
                     reads=[sq_b[i], ones_b], writes=([PSb[ps_sum]] if (c == 0 or c == C - 1) else ()))
            k.op(act, lambda: nc.scalar.activation(out=sd[:], in_=PSt[ps_sum][:], func=AF.Sqrt,
                                                   bias=epsc[:, 0:1], scale=1.0 / dn),
                 reads=[PSb[ps_sum], epsc_b], writes=[sd_b])
            k.op(dve, lambda: nc.vector.reciprocal(out=rstd[:], in_=sd[:]), reads=[sd_b], writes=[rstd_b])
            for c in range(C):
                k.op(dve, lambda c=c: nc.vector.scalar_tensor_tensor(
                    out=outs[c], in0=srcs[c], scalar=vecs[:, gcol + c:gcol + c + 1], in1=rstd[:],
                    op0=ALU.mult, op1=ALU.mult),
                    reads=[src_bufs[c], rstd_b, vecs_b], writes=[out_bufs[c]])

        def mmgroup(ps, items, m=128):
            n = len(items)
            for i, (l, r, rb) in enumerate(items):
                first, last = (i == 0), (i == n - 1)
                k.op(pe, lambda l=l, r=r, first=first, last=last: nc.tensor.matmul(
                    PSt[ps][0:m, :], l, r, start=first, stop=last),
                    reads=rb, writes=([PSb[ps]] if (first or last) else ()), inc=last)

        def ffn(ws, i_g, i_u, i_d, gcol):
            rmsnorm([xT[:, c, :] for c in range(DC)], xT_b, gcol, float(D),
                    [hT[:, c, :] for c in range(DC)], hT_b, 6)
            for fb in range(11):
                wg, wgb = ws.fetch(i_g[fb])
                wg3 = wg[:].rearrange("p (c f) -> p c f", c=16, f=512)
                for j in range(4):
                    fc = fb * 4 + j
                    pg = fc % 2
                    mmgroup(pg, [(wg3[:, c, j * 128:(j + 1) * 128], hT[:, c, :], [wgb, hT_b[c]]) for c in range(DC)])
                    k.op(act, lambda pg=pg, j=j: nc.scalar.activation(out=sgb[j][:], in_=PSt[pg][:], func=AF.Silu),
                         reads=[PSb[pg]], writes=[sgb_b[j]])
                wu, wub = ws.fetch(i_u[fb])
                wu3 = wu[:].rearrange("p (c f) -> p c f", c=16, f=512)
                for j in range(4):
                    fc = fb * 4 + j
                    pu = 2 + fc % 2
                    mmgroup(pu, [(wu3[:, c, j * 128:(j + 1) * 128], hT[:, c, :], [wub, hT_b[c]]) for c in range(DC)])
                    k.op(dve, lambda pu=pu, j=j, fc=fc: nc.vector.tensor_tensor(
                        out=mid[:, fc, :], in0=PSt[pu][:], in1=sgb[j][:], op=ALU.mult),
                        reads=[PSb[pu], sgb_b[j]], writes=[mid_b[fc]])
            for oc in range(DC):
                wd, wdb = ws.fetch(i_d[oc])
                wd3 = wd[:, 0:FC * 128].rearrange("p (c f) -> p c f", c=FC, f=128)
                pd = 4 + oc % 2
                mmgroup(pd, [(wd3[:, fc, :], mid[:, fc, :], [wdb, mid_b[fc]]) for fc in range(FC)])
                k.op(dve, lambda pd=pd, oc=oc: nc.vector.scalar_tensor_tensor(
                    out=xT[:, oc, :], in0=PSt[pd][:], scalar=0.5, in1=xT[:, oc, :], op0=ALU.mult, op1=ALU.add),
                    reads=[PSb[pd], xT_b[oc]], writes=[xT_b[oc]])

        def stage_store(dst_ap, fn_make, reads):
            i = ctr["stg"] % 4
            ctr["stg"] += 1
            fn_make(stg[i], stg_b[i], reads)
            k.dma(pool, stg_d[i], dst_ap, stg[i][:], reads=[stg_b[i]])

        ws1 = WStream(k, ring, ring_b, ring_d)
        p1_idx = []
        for t in range(NT):
            ig = [None] * 11
            iu = [None] * 11
            for b in range(11):
                ig[b] = ws1.add(s_wg[0][b], SLOT)
                iu[b] = ws1.add(s_wu[0][b], SLOT)
            idn = [ws1.add(s_wd[0][b], FC * 128) for b in range(16)]
            iin = [ws1.add(s_win[b], SLOT) for b in range(17)]
            iuq = [ws1.add(s_wuq[b], SLOT) for b in range(2)]
            iin.append(ws1.add(s_win[17], SLOT))
            ikpe = ws1.add(s_wkpe, 16 * 128)
            iukv = [ws1.add(s_wukv[b], SLOT) for b in range(2)]
            p1_idx.append((ig, iu, idn, iin, ikpe, iuq, iukv))

        for t in range(NT):
            ig, iu, idn, iin, ikpe, iuq, iukv = p1_idx[t]
            tok = slice(t * T, (t + 1) * T)
            k.dma(pool, d_x, xT[:], xT_in.rearrange("(c p) s -> p c s", p=128)[:, :, tok], writes=xT_b)
            k.dma(pool, d_rope, ropeC[:], ropeC_in[:, tok], writes=[rope_b])
            k.dma(pool, d_rope, ropeS[:], ropeS_in[:, tok], writes=[rope_b])
            if t == 0:
                prep_rest_a()
                ws1.gate = (d_prep2.key, d_prep2, d_prep2.total)
                prep_group_b()
            ffn(ws1, ig, iu, idn, V_G1)
            k.dma(pool, d_x, X1T[t].rearrange("p (c s) -> p c s", c=DC), xT[:], reads=xT_b)
            rmsnorm([xT[:, c, :] for c in range(DC)], xT_b, V_GM, float(D),
                    [hT[:, c, :] for c in range(DC)], hT_b, 6)

            def inproj_chunk(w3, wb, j, ps, m=128):
                mmgroup(ps, [(w3[:, c, j * m:(j + 1) * m], hT[:, c, :], [wb, hT_b[c]]) for c in range(DC)], m=m)

            for blk in range(16):
                w, wb = ws1.fetch(iin[blk])
                w3 = w[:].rearrange("p (c f) -> p c f", c=16, f=512)
                for j in range(4):
                    ps = ctr["ps"] % 4
                    ctr["ps"] += 1
                    inproj_chunk(w3, wb, j, ps)
                    ch = (blk % 4) * 4 + j if blk < 8 else (blk - 8) * 4 + j
                    if blk < 4:
                        def mk(st, stb, rd, ps=ps):
                            k.op(act, lambda: nc.scalar.activation(out=st[:], in_=PSt[ps][:], func=AF.Copy),
                                 reads=[PSb[ps]], writes=[stb])
                        stage_store(XREC[t][:, ch, :], mk, None)
                    elif blk < 8:
                        def mk(st, stb, rd, ps=ps):
                            ti = ctr["tmp"] % 4
                            ctr["tmp"] += 1
                            k.op(act, lambda: nc.scalar.activation(out=tmp[ti][:], in_=PSt[ps][:], func=AF.Square),
                                 reads=[PSb[ps]], writes=[tmp_b[ti]])
                            k.op(dve, lambda: nc.vector.tensor_scalar(
                                out=tmp[ti][:], in0=tmp[ti][:], scalar1=0.044715, scalar2=1.0,
                                op0=ALU.mult, op1=ALU.add), reads=[tmp_b[ti]], writes=[tmp_b[ti]])
                            k.op(dve, lambda: nc.vector.tensor_tensor(
                                out=tmp[ti][:], in0=PSt[ps][:], in1=tmp[ti][:], op=ALU.mult),
                                reads=[PSb[ps], tmp_b[ti]], writes=[tmp_b[ti]])
                            k.op(act, lambda: nc.scalar.activation(out=tmp[ti][:], in_=tmp[ti][:], func=AF.Sigmoid,
                                                                   scale=1.5957691216057308),
                                 reads=[tmp_b[ti]], writes=[tmp_b[ti]])
                            k.op(dve, lambda: nc.vector.tensor_tensor(
                                out=st[:], in0=PSt[ps][:], in1=tmp[ti][:], op=ALU.mult),
                                reads=[PSb[ps], tmp_b[ti]], writes=[stb])
                        stage_store(GREC[t][:, ch, :], mk, None)
                    else:
                        def mk(st, stb, rd, ps=ps):
                            k.op(act, lambda: nc.scalar.activation(out=st[:], in_=PSt[ps][:], func=AF.Sigmoid),
                                 reads=[PSb[ps]], writes=[stb])
                        stage_store(GATES[t][:, ch, :], mk, None)

            w, wb = ws1.fetch(iin[16])
            w3 = w[:].rearrange("p (c f) -> p c f", c=16, f=512)
            for j in range(4):
                inproj_chunk(w3, wb, j, j)
            cqn = [mid[:, 36 + c, :] for c in range(4)]
            cqn_b = [mid_b[36 + c] for c in range(4)]
            rmsnorm([PSt[j][:] for j in range(4)], [PSb[j] for j in range(4)], V_GQ, 512.0, cqn, cqn_b, 6)
            wq0, wq0b, wq1, wq1b = ws1.fetch(iuq[0], 2)
            wq0_3 = wq0[:].rearrange("p (c x) -> p c x", c=4, x=2048)
            wq1_3 = wq1[:].rearrange("p (c x) -> p c x", c=4, x=2048)
            for h in range(NH):
                ps = 4 + h % 2
                mmgroup(ps, [(wq0_3[:, c, h * 128:(h + 1) * 128], cqn[c], [wq0b, cqn_b[c]]) for c in range(4)])
                k.op(act, lambda ps=ps, h=h: nc.scalar.activation(out=mid[:, h, :], in_=PSt[ps][:], func=AF.Copy),
                     reads=[PSb[ps]], writes=[mid_b[h]])
                pa, pb = 0 + 2 * (h % 2), 1 + 2 * (h % 2)
                mmgroup(pa, [(wq1_3[:, c, h * 64:(h + 1) * 64], cqn[c], [wq1b, cqn_b[c]]) for c in range(4)], m=64)
                mmgroup(pb, [(wq1_3[:, c, 1024 + h * 64:1024 + (h + 1) * 64], cqn[c], [wq1b, cqn_b[c]])
                             for c in range(4)], m=64)
                t0, t1 = (ctr["tmp"]) % 4, (ctr["tmp"] + 1) % 4
                ctr["tmp"] += 2
                k.op(dve, lambda pa=pa, t0=t0: nc.vector.tensor_tensor(
                    out=tmp[t0][0:64, :], in0=PSt[pa][0:64, :], in1=ropeC[:], op=ALU.mult),
                    reads=[PSb[pa], rope_b], writes=[tmp_b[t0]])
                k.op(dve, lambda pb=pb, t1=t1: nc.vector.tensor_tensor(
                    out=tmp[t1][0:64, :], in0=PSt[pb][0:64, :], in1=ropeS[:], op=ALU.mult),
                    reads=[PSb[pb], rope_b], writes=[tmp_b[t1]])
                k.op(dve, lambda t0=t0, t1=t1, h=h: nc.vector.tensor_tensor(
                    out=mid[0:64, 16 + h, :], in0=tmp[t0][0:64, :], in1=tmp[t1][0:64, :], op=ALU.add),
                    reads=[tmp_b[t0], tmp_b[t1]], writes=[mid_b[16 + h]])
            k.dma(pool, d_st, QN[t].rearrange("p (h s) -> p h s", h=NH), mid[:, 0:16, :], reads=mid_b[0:16])
            k.dma(pool, d_st, QPE[t].rearrange("p (h s) -> p h s", h=NH), mid[0:64, 16:32, :], reads=mid_b[16:32])

            w, wb = ws1.fetch(iin[17])
            w3 = w[:].rearrange("p (c f) -> p c f", c=16, f=512)
            for j in range(4):
                inproj_chunk(w3, wb, j, j)
            ckn = [mid[:, 40 + c, :] for c in range(4)]
            ckn_b = [mid_b[40 + c] for c in range(4)]
            rmsnorm([PSt[j][:] for j in range(4)], [PSb[j] for j in range(4)], V_GKV, 512.0, ckn, ckn_b, 6)
            wkp, wkpb = ws1.fetch(ikpe)
            wkp3 = wkp[:, 0:2048].rearrange("p (c f) -> p c f", c=16, f=128)
            mmgroup(0, [(wkp3[:, c, 0:64], hT[:, c, :], [wkpb, hT_b[c]]) for c in range(DC)], m=64)
            mmgroup(1, [(wkp3[:, c, 64:128], hT[:, c, :], [wkpb, hT_b[c]]) for c in range(DC)], m=64)
            t0, t1 = (ctr["tmp"]) % 4, (ctr["tmp"] + 1) % 4
            ctr["tmp"] += 2
            k.op(dve, lambda: nc.vector.tensor_tensor(out=tmp[t0][0:64, :], in0=PSt[0][0:64, :], in1=ropeC[:],
                                                      op=ALU.mult), reads=[PSb[0], rope_b], writes=[tmp_b[t0]])
            k.op(dve, lambda: nc.vector.tensor_tensor(out=tmp[t1][0:64, :], in0=PSt[1][0:64, :], in1=ropeS[:],
                                                      op=ALU.mult), reads=[PSb[1], rope_b], writes=[tmp_b[t1]])
            k.op(dve, lambda: nc.vector.tensor_tensor(out=mid[0:64, 35, :], in0=tmp[t0][0:64, :],
                                                      in1=tmp[t1][0:64, :], op=ALU.add),
                 reads=[tmp_b[t0], tmp_b[t1]], writes=[mid_b[35]])
            k.dma(pool, d_st, KPE[:, tok], mid[0:64, 35, :], reads=[mid_b[35]])
            wk, wkb = ws1.fetch(iukv[0])
            wk3 = wk[:].rearrange("p (c x) -> p c x", c=4, x=2048)
            for h in range(NH):
                ps = 4 + h % 2
                mmgroup(ps, [(wk3[:, c, h * 128:(h + 1) * 128], ckn[c], [wkb, ckn_b[c]]) for c in range(4)])
                k.op(act, lambda ps=ps, h=h: nc.scalar.activation(out=mid[:, h, :], in_=PSt[ps][:], func=AF.Copy),
                     reads=[PSb[ps]], writes=[mid_b[h]])
            k.dma(pool, d_st, KT.rearrange("h p s -> p h s")[:, :, tok], mid[:, 0:16, :], reads=mid_b[0:16])
            wv, wvb = ws1.fetch(iukv[1])
            wv3 = wv[:].rearrange("p (c x) -> p c x", c=4, x=2048)
            for tc in range(4):
                for g in range(4):
                    ps = (tc * 4 + g) % 2 + 2
                    mmgroup(ps, [(ckn[c][:, tc * 128:(tc + 1) * 128], wv3[:, c, g * 512:(g + 1) * 512],
                                  [wvb, ckn_b[c]]) for c in range(4)])
                    k.op(dve, lambda ps=ps, tc=tc, g=g: nc.vector.tensor_copy(
                        out=mid[:, 16 + tc * 4 + g, :], in_=PSt[ps][:]),
                        reads=[PSb[ps]], writes=[mid_b[16 + tc * 4 + g]])
                k.dma(pool, d_st,
                      VS.rearrange("h p (kc e) -> p kc h e", e=128)[:, t * 4 + tc, :, :],
                      mid[:, 16 + tc * 4:20 + tc * 4, :].rearrange("p g (hh e) -> p (g hh) e", e=128),
                      reads=mid_b[16 + tc * 4:20 + tc * 4])
        k.barrier()
        es1.close()
        if stop == "p1":
            return nc

        es2 = contextlib.ExitStack()

        def sb2(name, shape, dt):
            return es2.enter_context(nc.sbuf_tensor(name, list(shape), dt))

        xr = sb2("xr", [128, S + 3], F32)
        xr_b = k.buf()
        xc = sb2("xc", [128, S], F32)
        xc_b = k.buf()
        xcb = sb2("xcb", [128, S], BF16)
        xcb_b = k.buf()
        xc_pb = k.bufs_n(NP)
        xcb_pb = k.bufs_n(NP)
        hf = sb2("hf", [128, S], F32)
        hf_b = k.buf()
        rgw = sb2("rgw", [128, 2, 2, 16, 128], BF16)
        rgw_b = k.buf()
        rbL = [sb2("rbuf%d" % i, [128, PT], F32) for i in range(2)]
        rbB = k.bufs_n(2)
        ibL = [sb2("ibuf%d" % i, [128, PT], F32) for i in range(2)]
        ibB = k.bufs_n(2)
        abL = [sb2("abuf%d" % i, [128, PT], F32) for i in range(2)]
        abB = k.bufs_n(2)
        ubL = [sb2("ubuf%d" % i, [128, PT], F32) for i in range(2)]
        ubB = k.bufs_n(2)
        hbL = [sb2("hbuf%d" % i, [128, PT], F32) for i in range(2)]
        hbB = k.bufs_n(2)
        ggL = [sb2("ggbuf%d" % i, [128, PT], F32) for i in range(2)]
        ggB = k.bufs_n(2)
        hgL = [sb2("hgbuf%d" % i, [128, PT], BF16) for i in range(2)]
        hgB = k.bufs_n(2)
        onec = sb2("onec", [128, 1], F32)
        onec_b = k.buf()
        k.op(dve, lambda: nc.vector.memset(onec[:], 1.0), writes=[onec_b])
        d_gg = [k.dsem("gg%d" % i) for i in range(2)]
        d_hg = [k.dsem("hg%d" % i) for i in range(2)]
        carry = sb2("carry", [128, 1], F32)
        carry_b = k.buf()
        cv = sb2("cv", [128, 3, 32], F32)
        cv_b = k.buf()
        sw_ = [sb2("swk%d" % i, [128, 32], F32) for i in range(4)]
        sw_b = k.buf()
        d_p2 = k.dsem("p2")
        d_p2s = k.dsem("p2s")
        d_rgw = k.dsem("rgw")

        for dd in range(2):
            k.dma(pool, d_rgw, rgw[:, 0, dd, :, :], rg_wa[dd].rearrange("n k j -> k n j"), writes=[rgw_b])
            k.dma(pool, d_rgw, rgw[:, 1, dd, :, :], rg_wi[dd].rearrange("n k j -> k n j"), writes=[rgw_b])
        k.op(pool, lambda: nc.gpsimd.memset(xr[:, 0:2], 0.0), writes=[xr_b])
        k.op(pool, lambda: nc.gpsimd.memset(xr[:, S + 2:S + 3], 0.0), writes=[xr_b])
        lam = vecs[:, V_LAM:V_LAM + 32]
        m_, e_, z_, z2_ = sw_[0], sw_[1], sw_[2], sw_[3]
        p_ = cv[:, 2, :]

        def vop(fn):
            k.op(dve, fn, reads=[vecs_b, sw_b, cv_b], writes=[sw_b, cv_b])
        vop(lambda: nc.vector.tensor_scalar(out=m_[:], in0=lam, scalar1=-1.0, scalar2=None, op0=ALU.mult))
        vop(lambda: nc.vector.tensor_tensor(out=m_[:], in0=m_[:], in1=lam, op=ALU.max))
        k.op(act, lambda: nc.scalar.activation(out=e_[:], in_=m_[:], func=AF.Exp, scale=-1.0),
             reads=[sw_b], writes=[sw_b])
        vop(lambda: nc.vector.tensor_scalar(out=z_[:], in0=e_[:], scalar1=2.0, scalar2=None, op0=ALU.add))
        vop(lambda: nc.vector.reciprocal(out=z_[:], in_=z_[:]))
        vop(lambda: nc.vector.tensor_tensor(out=z_[:], in0=z_[:], in1=e_[:], op=ALU.mult))
        vop(lambda: nc.vector.tensor_tensor(out=z2_[:], in0=z_[:], in1=z_[:], op=ALU.mult))
        vop(lambda: nc.vector.memset(p_, 1.0 / 17.0))
        for n in (15, 13, 11, 9, 7, 5, 3, 1):
            vop(lambda: nc.vector.tensor_tensor(out=p_, in0=p_, in1=z2_[:], op=ALU.mult))
            vop(lambda n=n: nc.vector.tensor_scalar(out=p_, in0=p_, scalar1=1.0 / n, scalar2=None, op0=ALU.add))
        vop(lambda: nc.vector.tensor_tensor(out=p_, in0=p_, in1=z_[:], op=ALU.mult))
        vop(lambda: nc.vector.tensor_scalar(out=m_[:], in0=lam, scalar1=-1.0, scalar2=0.0, op0=ALU.mult, op1=ALU.max))
        vop(lambda: nc.vector.scalar_tensor_tensor(out=m_[:], in0=p_, scalar=2.0, in1=m_[:], op0=ALU.mult, op1=ALU.add))
        vop(lambda: nc.vector.tensor_scalar(out=cv[:, 0, :], in0=m_[:], scalar1=-8.0, scalar2=None, op0=ALU.mult))
        vop(lambda: nc.vector.tensor_scalar(out=cv[:, 1, :], in0=m_[:], scalar1=-16.0, scalar2=None, op0=ALU.mult))

        for c in range(DC):
            k.dma(sp, d_p2, xr[:, 2:S + 2].rearrange("p (n s) -> p n s", n=NT),
                  XREC[:, :, c, :].rearrange("n p s -> p n s"), writes=[xr_b])
            def emit_conv(pc):
                o = pc * PT
                k.op(dve, lambda: nc.vector.tensor_scalar(
                    out=xc[:, o:o + PT], in0=xr[:, o:o + PT], scalar1=vecs[:, V_CW + c:V_CW + c + 1],
                    scalar2=vecs[:, V_CB + c:V_CB + c + 1], op0=ALU.mult, op1=ALU.add),
                    reads=[xr_b, vecs_b], writes=[xc_pb[pc]])
                for kk in range(1, 4):
                    k.op(dve, lambda kk=kk: nc.vector.scalar_tensor_tensor(
                        out=xc[:, o:o + PT], in0=xr[:, o + kk:o + kk + PT],
                        scalar=vecs[:, V_CW + kk * 16 + c:V_CW + kk * 16 + c + 1], in1=xc[:, o:o + PT],
                        op0=ALU.mult, op1=ALU.add),
                        reads=[xr_b, vecs_b, xc_pb[pc]], writes=[xc_pb[pc]])
                k.op(pool, lambda: nc.gpsimd.tensor_copy(out=xcb[:, o:o + PT], in_=xc[:, o:o + PT]),
                     reads=[xc_pb[pc]], writes=[xcb_pb[pc]])

            for pc in range(min(2, NP)):
                emit_conv(pc)
            for d in range(2):
                pieces = list(range(NP)) if d == 0 else list(range(NP - 1, -1, -1))
                for pi, pc in enumerate(pieces):
                    o = pc * PT
                    pp = ctr["pp"] % 2
                    ctr["pp"] += 1
                    rb_, rb_b, ib_, ib_b = rbL[pp], rbB[pp], ibL[pp], ibB[pp]
                    ab_, ab_b, ub_, ub_b = abL[pp], abB[pp], ubL[pp], ubB[pp]
                    hb_, hb_b, gg_, gg_b, hg_, hg_b = hbL[pp], hbB[pp], ggL[pp], ggB[pp], hgL[pp], hgB[pp]
                    pa0 = pp * 4
                    if d == 0 and pi + 2 < NP:
                        emit_conv(pi + 2)
                    for sub in range(NSUB):
                        rhs = xcb[:, o + sub * 512:o + (sub + 1) * 512]
                        mmgroup(pa0 + sub, [(rgw[:, 0, d, c, :], rhs, [rgw_b, xcb_pb[pc]])])
                        mmgroup(pa0 + 2 + sub, [(rgw[:, 1, d, c, :], rhs, [rgw_b, xcb_pb[pc]])])
                    bcol = d * 16 + c
                    for sub in range(NSUB):
                        k.op(act, lambda sub=sub: nc.scalar.activation(
                            out=rb_[:, sub * 512:(sub + 1) * 512], in_=PSt[pa0 + sub][:], func=AF.Sigmoid,
                            bias=vecs[:, V_BA + bcol:V_BA + bcol + 1], scale=1.0),
                            reads=[PSb[pa0 + sub], vecs_b], writes=[rb_b])
                    for sub in range(NSUB):
                        k.op(act, lambda sub=sub: nc.scalar.activation(
                            out=ib_[:, sub * 512:(sub + 1) * 512], in_=PSt[pa0 + 2 + sub][:], func=AF.Sigmoid,
                            bias=vecs[:, V_BI + bcol:V_BI + bcol + 1], scale=1.0),
                            reads=[PSb[pa0 + 2 + sub], vecs_b], writes=[ib_b])
                    k.op(act, lambda: nc.scalar.activation(out=ab_[:], in_=rb_[:], func=AF.Exp,
                                                           scale=cv[:, 0, bcol:bcol + 1]),
                         reads=[rb_b, cv_b], writes=[ab_b])
                    k.op(act, lambda: nc.scalar.activation(out=ub_[:], in_=rb_[:], func=AF.Exp,
                                                           scale=cv[:, 1, bcol:bcol + 1]),
                         reads=[rb_b, cv_b], writes=[ub_b])
                    k.op(act, lambda: nc.scalar.activation(out=ub_[:], in_=ub_[:], func=AF.Sqrt,
                                                           bias=onec[:, 0:1], scale=-1.0),
                         reads=[ub_b, onec_b], writes=[ub_b])
                    k.op(dve, lambda o=o: nc.vector.tensor_tensor(out=ib_[:], in0=ib_[:], in1=xc[:, o:o + PT],
                                                                  op=ALU.mult),
                         reads=[ib_b, xc_pb[pc]], writes=[ib_b])
                    k.op(dve, lambda: nc.vector.tensor_tensor(out=ub_[:], in0=ub_[:], in1=ib_[:], op=ALU.mult),
                         reads=[ub_b, ib_b], writes=[ub_b])
                    if d == 0:
                        init = 0.0 if pi == 0 else hf[:, o - 1:o]
                        k.op(dve, lambda o=o, init=init: nc.vector.tensor_tensor_scan(
                            out=hf[:, o:o + PT], data0=ab_[:], data1=ub_[:], initial=init,
                            op0=ALU.mult, op1=ALU.add),
                            reads=[ab_b, ub_b, hf_b], writes=[hf_b])
                    else:
                        def rev(tn, o0, n):
                            ap = tn[:, o0:o0 + n]
                            (ps_, pc_), (fs_, fc_) = ap.ap
                            return bass.AP(ap.tensor, ap.offset + (fc_ - 1) * fs_, [[ps_, pc_], [-fs_, fc_]])
                        init = 0.0 if pi == 0 else carry[:, 0:1]
                        k.op(dve, lambda init=init: nc.vector.tensor_tensor_scan(
                            out=rev(hb_, 0, PT), data0=rev(ab_, 0, PT), data1=rev(ub_, 0, PT), initial=init,
                            op0=ALU.mult, op1=ALU.add),
                            reads=[ab_b, ub_b, carry_b], writes=[hb_b])
                        k.op(dve, lambda: nc.vector.tensor_copy(out=carry[:], in_=hb_[:, 0:1]),
                             reads=[hb_b], writes=[carry_b])
                        k.dma(sp, d_gg[pp], gg_[:].rearrange("p (n s) -> p n s", n=PT // T),
                              GREC[pc * (PT // T):(pc + 1) * (PT // T), :, c, :].rearrange("n p s -> p n s"),
                              writes=[gg_b])
                        k.op(dve, lambda o=o: nc.vector.tensor_tensor(out=hb_[:], in0=hb_[:], in1=hf[:, o:o + PT],
                                                                      op=ALU.add),
                             reads=[hb_b, hf_b], writes=[hb_b])
                        k.op(dve, lambda: nc.vector.tensor_tensor(out=hg_[:], in0=hb_[:], in1=gg_[:], op=ALU.mult),
                             reads=[hb_b, gg_b], writes=[hg_b])
                        k.dma(pool, d_hg[pp],
                              HG[pc * (PT // T):(pc + 1) * (PT // T)].rearrange("n p (c s) -> p n c s", c=DC)[:, :, c, :],
                              hg_[:].rearrange("p (n s) -> p n s", n=PT // T), reads=[hg_b])
        k.barrier()
        es2.close()
        if stop == "p2":
            return nc

        es3 = contextlib.ExitStack()

        def sb3(name, shape, dt):
            return es3.enter_context(nc.sbuf_tensor(name, list(shape), dt))

        qn = sb3("qn", [128, NH, T], BF16)
        qn_b = k.buf()
        qpe = sb3("qpe", [128, NH, T], BF16)
        qpe_b = k.buf()
        kpe = sb3("kpe", [128, S], BF16)
        kpe_b = k.buf()
        k.op(dve, lambda: nc.vector.memset(qpe[64:128, :, :], 0.0), writes=[qpe_b])
        k.op(dve, lambda: nc.vector.memset(kpe[64:128, :], 0.0), writes=[kpe_b])
        Kb = [sb3("Kb%d" % i, [128, S], BF16) for i in range(2)]
        Kb_b = k.bufs_n(2)
        Vb = [sb3("Vb%d" % i, [128, NKC, 128], BF16) for i in range(2)]
        Vb_b = k.bufs_n(2)
        pT_ = [sb3("pT%d" % i, [128, T], BF16) for i in range(4)]
        pT_b = k.bufs_n(4)
        att = sb3("att", [128, NH, T], BF16)
        att_b = k.bufs_n(NH)
        rs = sb3("rs", [128, T], F32)
        rs_b = k.buf()
        accD = [[sb3("accD%d_%d" % (i, j), [128, T], F32) for j in range(2)] for i in range(2)]
        accD_b = [k.bufs_n(2) for i in range(2)]
        accP = [sb3("accP%d" % i, [128, T], F32) for i in range(2)]
        accP_b = k.bufs_n(2)
        d_q = k.dsem("q")
        d_kv = [k.dsem("kv%d" % i) for i in range(2)]
        d_att = k.dsem("att")
        k.dma(sp, d_q, kpe[0:64, :], KPE[:, :], writes=[kpe_b])
        LA = 2

        def load_kv(it):
            h = it % NH
            s = it % 2
            k.dma(sp, d_kv[s], Kb[s][:], KT[h], writes=[Kb_b[s]])
            k.dma(sp, d_kv[s], Vb[s][:].rearrange("p a e -> p (a e)"), VS[h], writes=[Vb_b[s]])

        load_kv(0)
        for t in range(NT):
            k.dma(sp, d_q, qn[:].rearrange("p h s -> p (h s)"), QN[t], writes=[qn_b])
            k.dma(sp, d_q, qpe[0:64, :, :].rearrange("p h s -> p (h s)"), QPE[t], writes=[qpe_b])
            for h in range(NH):
                it = t * NH + h
                if it + 1 < NT * NH:
                    load_kv(it + 1)
                s = it % 2
                po, psm = 3 + h % 2, 5 + h % 2
                nacc = 0
                for kc in range(NKC + LA):
                    if kc < NKC:
                        ps = kc % 3
                        pi = kc % 4
                        ks = slice(kc * 128, (kc + 1) * 128)
                        k.op(pe, lambda ps=ps, ks=ks: nc.tensor.matmul(
                            PSt[ps][:], Kb[s][:, ks], qn[:, h, :], start=True, stop=False),
                            reads=[Kb_b[s], qn_b], writes=[PSb[ps]], inc=False)
                        k.op(pe, lambda ps=ps, ks=ks: nc.tensor.matmul(
                            PSt[ps][:], kpe[:, ks], qpe[:, h, :], start=False, stop=True),
                            reads=[kpe_b, qpe_b], writes=[PSb[ps]])
                        k.op(act, lambda ps=ps, pi=pi: nc.scalar.activation(
                            out=pT_[pi][:], in_=PSt[ps][:], func=AF.Exp, scale=SCALE),
                            reads=[PSb[ps]], writes=[pT_b[pi]])
                        hp = h % 2
                        if kc % 8 == 7:
                            if kc == 7:
                                k.op(pool, lambda pi=pi: nc.gpsimd.tensor_copy(out=accP[hp][:], in_=pT_[pi][:]),
                                     reads=[pT_b[pi]], writes=[accP_b[hp]])
                            else:
                                k.op(pool, lambda pi=pi: nc.gpsimd.tensor_tensor(
                                    out=accP[hp][:], in0=accP[hp][:], in1=pT_[pi][:], op=ALU.add),
                                    reads=[pT_b[pi], accP_b[hp]], writes=[accP_b[hp]])
                        else:
                            ai = nacc % 2
                            if nacc < 2:
                                k.op(dve, lambda pi=pi, ai=ai: nc.vector.tensor_copy(out=accD[hp][ai][:], in_=pT_[pi][:]),
                                     reads=[pT_b[pi]], writes=[accD_b[hp][ai]])
                            else:
                                k.op(dve, lambda pi=pi, ai=ai: nc.vector.tensor_tensor(
                                    out=accD[hp][ai][:], in0=accD[hp][ai][:], in1=pT_[pi][:], op=ALU.add),
                                    reads=[pT_b[pi], accD_b[hp][ai]], writes=[accD_b[hp][ai]])
                            nacc += 1
                    if kc >= LA:
                        j = kc - LA
                        pj = j % 4
                        first, last = (j == 0), (j == NKC - 1)
                        k.op(pe, lambda j=j, pj=pj, first=first, last=last: nc.tensor.matmul(
                            PSt[po][:], Vb[s][:, j, :], pT_[pj][:], start=first, stop=last),
                            reads=[Vb_b[s], pT_b[pj]], writes=([PSb[po]] if (first or last) else ()), inc=last)
                hp = h % 2
                k.op(dve, lambda: nc.vector.tensor_tensor(out=accD[hp][0][:], in0=accD[hp][0][:], in1=accD[hp][1][:],
                                                          op=ALU.add),
                     reads=[accD_b[hp][0], accD_b[hp][1]], writes=[accD_b[hp][0]])
                k.op(dve, lambda: nc.vector.tensor_tensor(out=accD[hp][0][:], in0=accD[hp][0][:], in1=accP[hp][:],
                                                          op=ALU.add),
                     reads=[accD_b[hp][0], accP_b[hp]], writes=[accD_b[hp][0]])
                k.op(pe, lambda: nc.tensor.matmul(PSt[psm][:], ones_f[:], accD[hp][0][:], start=True, stop=True),
                     reads=[accD_b[hp][0], onesf_b], writes=[PSb[psm]])
                k.op(dve, lambda: nc.vector.reciprocal(out=rs[:], in_=PSt[psm][:]), reads=[PSb[psm]], writes=[rs_b])
                k.op(dve, lambda h=h: nc.vector.tensor_tensor(out=att[:, h, :], in0=PSt[po][:], in1=rs[:], op=ALU.mult),
                     reads=[PSb[po], rs_b], writes=[att_b[h]])
            k.dma(pool, d_att, ATT[t].rearrange("p (h s) -> p h s", h=NH), att[:], reads=att_b)
        k.barrier()
        es3.close()
        if stop == "p3a":
            return nc

        es1 = contextlib.ExitStack()
        xT = sb1("xTt2", [128, DC, T], F32)
        hT = sb1("hTt2", [128, DC, T], BF16)
        mid = sb1("midt2", [128, FC, T], BF16)
        ring = [sb1("ring2_%d" % i, [128, SLOT], BF16) for i in range(NSLOT)]
        sq = [sb1("sq2_%d" % i, [128, T], BF16) for i in range(2)]
        sd = sb1("sd2", [128, T], F32)
        rstd = sb1("rstd2", [128, T], F32)
        stg = [sb1("stg2_%d" % i, [128, T], F32) for i in range(4)]
        tmp = [sb1("tmp2_%d" % i, [128, T], F32) for i in range(4)]
        sgb = [sb1("sgb2_%d" % i, [128, T], F32) for i in range(4)]

        ws3 = WStream(k, ring, ring_b, ring_d)
        p3_idx = []
        for t in range(NT):
            ioa, ior = [None] * 4, [None] * 4
            for b in range(4):
                ioa[b] = ws3.add(s_woa[b], SLOT)
                ior[b] = ws3.add(s_wor[b], SLOT)
            iout = [ws3.add(s_wout[b], SLOT) for b in range(4)]
            ig = [None] * 11
            iu = [None] * 11
            for b in range(11):
                ig[b] = ws3.add(s_wg[1][b], SLOT)
                iu[b] = ws3.add(s_wu[1][b], SLOT)
            idn = [ws3.add(s_wd[1][b], FC * 128) for b in range(16)]
            p3_idx.append((ioa, ior, iout, ig, iu, idn))

        for t in range(NT):
            ioa, ior, iout, ig, iu, idn = p3_idx[t]
            tok = slice(t * T, (t + 1) * T)
            k.dma(pool, d_x, xT[:], X1T[t].rearrange("p (c s) -> p c s", c=DC), writes=xT_b)
            k.dma(pool, d_ld, mid[:, 0:16, :], ATT[t].rearrange("p (h s) -> p h s", h=NH), writes=mid_b[0:16])
            k.dma(pool, d_ld, mid[:, 16:32, :], HG[t].rearrange("p (c s) -> p c s", c=DC), writes=mid_b[16:32])
            for blk in range(4):
                wa, wab, wr, wrb = ws3.fetch(ioa[blk], 2)
                wa3 = wa[:].rearrange("p (c f) -> p c f", c=16, f=512)
                wr3 = wr[:].rearrange("p (c f) -> p c f", c=16, f=512)
                for j in range(4):
                    oc = blk * 4 + j
                    pa, pr = oc % 2, 2 + oc % 2
                    mmgroup(pa, [(wa3[:, c, j * 128:(j + 1) * 128], mid[:, c, :], [wab, mid_b[c]]) for c in range(16)])
                    mmgroup(pr, [(wr3[:, c, j * 128:(j + 1) * 128], mid[:, 16 + c, :], [wrb, mid_b[16 + c]])
                                 for c in range(16)])
                    sa, sr = (ctr["stg"]) % 4, (ctr["stg"] + 1) % 4
                    ctr["stg"] += 2
                    k.dma(pool, stg_d[sa], stg[sa][:], GATES[t][:, oc, :], writes=[stg_b[sa]])
                    k.dma(pool, stg_d[sr], stg[sr][:], GATES[t][:, 16 + oc, :], writes=[stg_b[sr]])
                    k.op(dve, lambda pa=pa, sa=sa: nc.vector.tensor_tensor(out=stg[sa][:], in0=PSt[pa][:], in1=stg[sa][:],
                                                                           op=ALU.mult),
                         reads=[PSb[pa], stg_b[sa]], writes=[stg_b[sa]])
                    k.op(dve, lambda pr=pr, sr=sr: nc.vector.tensor_tensor(out=stg[sr][:], in0=PSt[pr][:], in1=stg[sr][:],
                                                                           op=ALU.mult),
                         reads=[PSb[pr], stg_b[sr]], writes=[stg_b[sr]])
                    k.op(dve, lambda sa=sa, sr=sr, oc=oc: nc.vector.tensor_tensor(
                        out=hT[:, oc, :], in0=stg[sa][:], in1=stg[sr][:], op=ALU.add),
                        reads=[stg_b[sa], stg_b[sr]], writes=[hT_b[oc]])
            for blk in range(4):
                wo, wob = ws3.fetch(iout[blk])
                wo3 = wo[:].rearrange("p (c f) -> p c f", c=16, f=512)
                for j in range(4):
                    oc = blk * 4 + j
                    pd = 4 + oc % 2
                    mmgroup(pd, [(wo3[:, c, j * 128:(j + 1) * 128], hT[:, c, :], [wob, hT_b[c]]) for c in range(16)])
                    k.op(dve, lambda pd=pd, oc=oc: nc.vector.tensor_tensor(
                        out=xT[:, oc, :], in0=PSt[pd][:], in1=xT[:, oc, :], op=ALU.add),
                        reads=[PSb[pd], xT_b[oc]], writes=[xT_b[oc]])
            ffn(ws3, ig, iu, idn, V_G2)
            rmsnorm([xT[:, c, :] for c in range(DC)], xT_b, V_GF, float(D),
                    [xT[:, c, :] for c in range(DC)], xT_b, 6)
            k.dma(pool, d_x, yT_out.rearrange("(c p) s -> p c s", p=128)[:, :, tok], xT[:], reads=xT_b)
        k.barrier()
        es1.close()
    return nc


def _col(v):
    v = np.asarray(v, np.float32).reshape(-1, 128)
    return np.ascontiguousarray(v.T)


def host_inputs(inp, S):
    f = lambda a: np.ascontiguousarray(np.asarray(a, np.float32))
    w_in = f(inp["w_in"][0])
    w_uq = f(inp["w_uq"][0])
    kp = w_in[:, 1024:1088]
    w_kpesw = np.ascontiguousarray(np.concatenate([kp[:, 32:64], kp[:, 0:32]], axis=1))
    uq = w_uq.reshape(512, 16, 192)[:, :, 128:192]
    w_uqsw = np.ascontiguousarray(np.concatenate([uq[:, :, 32:64], uq[:, :, 0:32]], axis=2).reshape(512, 1024))
    vecs = np.zeros((128, NV), np.float32)
    vecs[:, V_G1:V_G1 + 16] = _col(inp["ffn1_norm"][0])
    vecs[:, V_GM:V_GM + 16] = _col(inp["mix_norm"][0])
    vecs[:, V_GQ:V_GQ + 4] = _col(inp["q_norm"][0])
    vecs[:, V_GKV:V_GKV + 4] = _col(inp["kv_norm"][0])
    vecs[:, V_G2:V_G2 + 16] = _col(inp["ffn2_norm"][0])
    vecs[:, V_GF:V_GF + 16] = _col(inp["final_norm"])
    for kk in range(4):
        vecs[:, V_CW + kk * 16:V_CW + kk * 16 + 16] = _col(inp["conv_w"][0][kk])
    vecs[:, V_CB:V_CB + 16] = _col(inp["conv_b"][0])
    for d in range(2):
        vecs[:, V_BA + d * 16:V_BA + d * 16 + 16] = _col(inp["rg_b_a"][0][d])
        vecs[:, V_BI + d * 16:V_BI + d * 16 + 16] = _col(inp["rg_b_i"][0][d])
        vecs[:, V_LAM + d * 16:V_LAM + d * 16 + 16] = _col(inp["rg_lambda"][0][d])
    pos = np.arange(S, dtype=np.float32)
    inv_freq = (np.float32(10000.0) ** (-np.arange(0, 64, 2, dtype=np.float32) / np.float32(64))).astype(np.float32)
    ang = (pos[:, None] * inv_freq[None, :]).astype(np.float32)
    cos = np.cos(ang).astype(np.float32).T
    sin = np.sin(ang).astype(np.float32).T
    ropeC = np.ascontiguousarray(np.concatenate([cos, cos], axis=0))
    ropeS = np.ascontiguousarray(np.concatenate([-sin, sin], axis=0))
    shared = {
        "ffn1_w_gate": f(inp["ffn1_w_gate"][0]), "ffn2_w_gate": f(inp["ffn2_w_gate"][0]),
        "ffn1_w_up": f(inp["ffn1_w_up"][0]), "ffn2_w_up": f(inp["ffn2_w_up"][0]),
        "ffn1_w_down": f(inp["ffn1_w_down"][0]), "ffn2_w_down": f(inp["ffn2_w_down"][0]),
        "w_in": w_in, "w_kpesw": w_kpesw, "w_uq": w_uq, "w_uqsw": w_uqsw, "w_ukv": f(inp["w_ukv"][0]),
        "w_o_attn": f(inp["w_o_attn"][0]), "w_o_rec": f(inp["w_o_rec"][0]), "w_out": f(inp["w_out"][0]),
        "rg_w_a": f(inp["rg_w_a"][0]), "rg_w_i": f(inp["rg_w_i"][0]),
        "vecs": vecs, "ropeC": ropeC, "ropeS": ropeS,
    }
    return shared


def kernel(**inputs):
    S = 8192
    xp = np.asarray(inputs["x_prompt"], np.float32)
    xs = np.asarray(inputs["x_sample"], np.float32)
    seqs = [xp[0]] + [xs[i] for i in range(4)]
    shared = host_inputs(inputs, S)
    nc = build(S)
    owner = [0, 1, 2, 4, 5]
    zero_x = np.zeros((D, S), np.float32)
    zero_shared = {name: np.zeros_like(a) for name, a in shared.items()}
    in_maps = []
    for c in range(8):
        if c in owner:
            m = dict(shared)
            m["xT"] = np.ascontiguousarray(seqs[owner.index(c)].T)
        else:
            m = dict(zero_shared)
            m["xT"] = zero_x
        in_maps.append(m)
    res = run_bass_kernel_spmd(nc, in_maps, core_ids=list(range(8)))
    outs = [np.ascontiguousarray(np.asarray(res.results[c]["yT"], np.float32).T) for c in owner]
    y_prompt = outs[0][None]
    y_sample = np.stack(outs[1:5], axis=0)
    return (y_prompt.astype(np.float32), y_sample.astype(np.float32))
```
